# Optimizing a Trainium2 kernel written in Bass

```python
import jax, jax.numpy as jnp
from jax import lax
import numpy as np

D_MODEL = 2048
BATCH = 4
SEQ = 2048
DEPTH = 2

MIX_WIDTH = D_MODEL
N_MIXERS = 4
GROUP_WIDTH = MIX_WIDTH // N_MIXERS
GMLP_CHUNK = 128
GMLP_GROUPS = 8
GMLP_GROUP_DIM = GROUP_WIDTH // GMLP_GROUPS
DIFF_HEAD_DIM = 64
DIFF_HEADS = GROUP_WIDTH // (2 * DIFF_HEAD_DIM)
ATTN_BLOCK = 128
DIL_HEAD_DIM = 64
DIL_HEADS = GROUP_WIDTH // DIL_HEAD_DIM
DIL_PATTERNS = ((128, 1), (512, 4), (2048, 16))
CONV_WIDTH = 31
CONV_CH = GROUP_WIDTH
D_FF = 4 * D_MODEL
N_IN_SPLITS = 10
IN_WIDTH = N_IN_SPLITS * GROUP_WIDTH
N_ALIBI_HEADS = DIFF_HEADS + DIL_HEADS
NORM_EPS = 1e-6

kernel_name = "hybrid_parallel_gmlp_diffattn_dilated_conformer"


def rmsnorm(x, g):
    xf = x.astype(jnp.float32)
    y = xf * lax.rsqrt(jnp.mean(xf * xf, axis=-1, keepdims=True) + NORM_EPS)
    return (y * g.astype(jnp.float32)).astype(x.dtype)


def layernorm_noparam(x):
    xf = x.astype(jnp.float32)
    mu = jnp.mean(xf, axis=-1, keepdims=True)
    xc = xf - mu
    y = xc * lax.rsqrt(jnp.mean(xc * xc, axis=-1, keepdims=True) + NORM_EPS)
    return y.astype(x.dtype)


def alibi_slopes():
    i = jnp.arange(1, N_ALIBI_HEADS + 1, dtype=jnp.float32)
    s = 2.0 ** (-8.0 * i / N_ALIBI_HEADS)
    diff_idx = np.arange(0, N_ALIBI_HEADS, 3)
    dil_idx = np.array([j for j in range(N_ALIBI_HEADS) if j % 3 != 0])
    return s[diff_idx], s[dil_idx]


def gmlp_mixer(u, v, w_s, b_s):
    B, S, _ = u.shape
    u = jax.nn.gelu(u)
    v = layernorm_noparam(jax.nn.gelu(v))
    nc = S // GMLP_CHUNK
    vc = v.reshape(B, nc, GMLP_CHUNK, GMLP_GROUPS, GMLP_GROUP_DIM)
    causal = jnp.tril(jnp.ones((GMLP_CHUNK, GMLP_CHUNK), dtype=bool))
    w = jnp.where(causal[None], w_s, jnp.zeros_like(w_s))
    z = jnp.einsum('gts,bcsgd->bctgd', w, vc) + b_s.T[None, None, :, :, None]
    return u * z.reshape(B, S, GROUP_WIDTH)


def diff_attention(q, k, v, lam, lam_init, subln_g, slopes):
    B, S = q.shape[:2]
    nb = S // ATTN_BLOCK
    scale = DIFF_HEAD_DIM ** -0.5
    key_pos = jnp.arange(S)
    qb = q.reshape(B, nb, ATTN_BLOCK, DIFF_HEADS, 2, DIFF_HEAD_DIM).transpose(1, 0, 2, 3, 4, 5)

    def block(args):
        qi, start = args
        s = jnp.einsum('bqhcd,bkhcd->bhcqk', qi, k).astype(jnp.float32) * scale
        dist = (start + jnp.arange(ATTN_BLOCK))[:, None] - key_pos[None, :]
        bias = -slopes[:, None, None, None] * dist.astype(jnp.float32)
        s = jnp.where(dist >= 0, s + bias, -jnp.inf)
        p = jax.nn.softmax(s, axis=-1)
        a = p[:, :, 0] - lam * p[:, :, 1]
        return jnp.einsum('bhqk,bkhe->bqhe', a.astype(v.dtype), v)

    o = lax.map(block, (qb, jnp.arange(nb) * ATTN_BLOCK))
    o = o.transpose(1, 0, 2, 3, 4).reshape(B, S, DIFF_HEADS, 2 * DIFF_HEAD_DIM)
    o = rmsnorm(o, subln_g) * (1.0 - lam_init)
    return o.reshape(B, S, GROUP_WIDTH)


def dilated_attention(q, k, v, slopes):
    B, S, H, dh = q.shape
    scale = dh ** -0.5
    results = []
    for window, dil in DIL_PATTERNS:
        n = window // dil
        seg = n * dil
        sp = -(-S // seg) * seg
        nb = sp // seg

        def to_classes(t):
            t = jnp.pad(t, ((0, 0), (0, sp - S), (0, 0), (0, 0)))
            t = t.reshape(B, sp // dil, dil, H, dh).transpose(0, 2, 3, 1, 4)
            return t.reshape(B, dil, H, nb, n, dh)

        def with_prev(t):
            prev = jnp.pad(t, ((0, 0), (0, 0), (0, 0), (1, 0), (0, 0), (0, 0)))[:, :, :, :-1]
            return jnp.concatenate([prev, t], axis=4)

        def from_classes(t):
            e = t.shape[-1]
            t = t.reshape(B, dil, H, sp // dil, e).transpose(0, 3, 1, 2, 4).reshape(B, sp, H, e)
            return t[:, :S]

        qc = to_classes(q)
        kb = with_prev(to_classes(k))
        vb = with_prev(to_classes(v))
        s = jnp.einsum('brhcqd,brhckd->brhcqk', qc, kb).astype(jnp.float32) * scale
        step = n + jnp.arange(n)[:, None] - jnp.arange(2 * n)[None, :]
        has_key = (jnp.arange(nb)[:, None, None] > 0) | (jnp.arange(2 * n)[None, None, :] >= n)
        valid = (step >= 0) & (step <= n) & has_key
        bias = -slopes[:, None, None, None] * (step * dil).astype(jnp.float32)
        s = jnp.where(valid, s + bias, -jnp.inf)
        m = jnp.max(s, axis=-1, keepdims=True)
        p = jnp.exp(s - m)
        l = jnp.sum(p, axis=-1, keepdims=True)
        o = jnp.einsum('brhcqk,brhckd->brhcqd', p.astype(v.dtype), vb).astype(jnp.float32)
        results.append((from_classes(m), from_classes(l), from_classes(o)))
    m_all = results[0][0]
    for m_i, _, _ in results[1:]:
        m_all = jnp.maximum(m_all, m_i)
    num = sum(jnp.exp(m_i - m_all) * o_i for m_i, _, o_i in results)
    den = sum(jnp.exp(m_i - m_all) * l_i for m_i, l_i, _ in results)
    return (num / den).astype(q.dtype).reshape(B, S, GROUP_WIDTH)


def conformer_conv(a, gate, w_dw, b_dw, norm_g):
    h = a * jax.nn.sigmoid(gate)
    h = lax.conv_general_dilated(
        h, w_dw[:, None, :].astype(h.dtype), window_strides=(1,), padding=[(CONV_WIDTH - 1, 0)],
        dimension_numbers=('NWC', 'WIO', 'NWC'), feature_group_count=CONV_CH)
    h = rmsnorm(h + b_dw, norm_g)
    return jax.nn.silu(h)


def setup_inputs(seed: int = 0) -> dict:
    key = jax.random.key(seed)
    ks = jax.random.split(key, 20)
    L, f32 = DEPTH, jnp.float32
    nrm = lambda k, shape, scale: jax.random.normal(k, shape, f32) * scale
    gain = lambda k, shape: 1.0 + 0.1 * jax.random.normal(k, shape, f32)
    return {
        "x": jax.random.normal(ks[0], (BATCH, SEQ, D_MODEL), f32),
        "g_mix_pre": gain(ks[1], (L, D_MODEL)),
        "g_mix_post": gain(ks[2], (L, D_MODEL)),
        "w_in": nrm(ks[3], (L, D_MODEL, IN_WIDTH), D_MODEL ** -0.5),
        "gmlp_w": nrm(ks[4], (L, GMLP_GROUPS, GMLP_CHUNK, GMLP_CHUNK), GMLP_CHUNK ** -0.5),
        "gmlp_b": gain(ks[5], (L, GMLP_GROUPS, GMLP_CHUNK)),
        "diff_lam": nrm(ks[6], (L, 4, DIFF_HEAD_DIM), 0.1),
        "diff_subln": gain(ks[7], (L, 2 * DIFF_HEAD_DIM)),
        "conv_w": nrm(ks[8], (L, CONV_WIDTH, CONV_CH), CONV_WIDTH ** -0.5),
        "conv_b": nrm(ks[9], (L, CONV_CH), 0.02),
        "conv_norm": gain(ks[10], (L, CONV_CH)),
        "w_out": nrm(ks[11], (L, MIX_WIDTH, D_MODEL), MIX_WIDTH ** -0.5),
        "g_ffn_pre": gain(ks[12], (L, D_MODEL)),
        "g_ffn_post": gain(ks[13], (L, D_MODEL)),
        "w_ff1": nrm(ks[14], (L, D_MODEL, D_FF), D_MODEL ** -0.5),
        "w_ff2": nrm(ks[15], (L, D_FF, D_MODEL), D_FF ** -0.5),
    }


def reference(x, g_mix_pre, g_mix_post, w_in, gmlp_w, gmlp_b, diff_lam, diff_subln,
              conv_w, conv_b, conv_norm, w_out, g_ffn_pre, g_ffn_post, w_ff1, w_ff2):
    B, S, _ = x.shape
    slopes_diff, slopes_dil = alibi_slopes()
    for li in range(DEPTH):
        h = rmsnorm(x, g_mix_pre[li])
        proj = jnp.einsum('bsd,de->bse', h, w_in[li])
        a_u, a_v, b_q, b_k, b_v, c_q, c_k, c_v, d_a, d_g = jnp.split(proj, N_IN_SPLITS, axis=-1)

        out_a = gmlp_mixer(a_u, a_v, gmlp_w[li], gmlp_b[li])

        lam_init = 0.8 - 0.6 * float(np.exp(-0.3 * li))
        lp = diff_lam[li].astype(jnp.float32)
        lam = jnp.exp(jnp.sum(lp[0] * lp[1])) - jnp.exp(jnp.sum(lp[2] * lp[3])) + lam_init
        out_b = diff_attention(
            b_q.reshape(B, S, DIFF_HEADS, 2, DIFF_HEAD_DIM),
            b_k.reshape(B, S, DIFF_HEADS, 2, DIFF_HEAD_DIM),
            b_v.reshape(B, S, DIFF_HEADS, 2 * DIFF_HEAD_DIM),
            lam, lam_init, diff_subln[li], slopes_diff)

        out_c = dilated_attention(
            c_q.reshape(B, S, DIL_HEADS, DIL_HEAD_DIM),
            c_k.reshape(B, S, DIL_HEADS, DIL_HEAD_DIM),
            c_v.reshape(B, S, DIL_HEADS, DIL_HEAD_DIM),
            slopes_dil)

        out_d = conformer_conv(d_a, d_g, conv_w[li], conv_b[li], conv_norm[li])

        mixed = jnp.concatenate([out_a, out_b, out_c, out_d], axis=-1)
        x = x + rmsnorm(jnp.einsum('bse,ed->bsd', mixed, w_out[li]), g_mix_post[li])

        h = rmsnorm(x, g_ffn_pre[li])
        f = jnp.square(jax.nn.relu(jnp.einsum('bsd,df->bsf', h, w_ff1[li])))
        x = x + rmsnorm(jnp.einsum('bsf,fd->bsd', f, w_ff2[li]), g_ffn_post[li])
    return x
```

```python
import contextlib
import math
import numpy as np
import concourse.bass as bass
import concourse.mybir as mybir
from concourse.bass_utils import run_bass_kernel_spmd

F32 = mybir.dt.float32
BF16 = mybir.dt.bfloat16
AF = mybir.ActivationFunctionType
ALU = mybir.AluOpType
AX = mybir.AxisListType

D = 2048
S = 2048
NB = S // 128
SL = S // 2
NBL = SL // 128
PAIRS = [[0, 1], [2, 3], [4, 5], [6, 7]]
DEPTH = 2
GW = 512
INW = 10 * GW
DFF = 4 * D
EPS = 1e-6
N_ALIBI = 12
CONVW = 31


class Buf:
    def __init__(self, name, ap=None, sem=None):
        self.name = name
        self.ap = ap
        self.sem = sem
        self.dcount = 0
        self.w = {}
        self.r = {}
        self.multi = False

    def __getitem__(self, k):
        return self.ap[k]


def _merge(dst, evs):
    for k, (s, v) in evs.items():
        if k not in dst or dst[k][1] < v:
            dst[k] = (s, v)


class Sched:
    ENG = ("pe", "act", "dve", "pool", "sp")

    def __init__(self, nc, stack):
        self.nc = nc
        self.stack = stack
        self.prog = {e: [] for e in self.ENG}
        self.esem = {e: stack.enter_context(nc.semaphore("es_" + e)) for e in self.ENG}
        self.ecount = {e: 0 for e in self.ENG}
        self.waited = {e: {} for e in self.ENG}
        self.allev = {}
        self.nsem = 5
        self.trace = {e: [] for e in self.ENG}
        self.free_sems = []

    def newsem(self, name):
        if self.free_sems:
            return self.free_sems.pop()
        self.nsem += 1
        return self.stack.enter_context(self.nc.semaphore(name)), 0

    def release(self, bufs):
        for b in bufs:
            if b.sem is not None:
                self.free_sems.append((b.sem, b.dcount))
                b.sem = None

    def _deps(self, eng, reads, writes):
        deps = {}
        for b in reads:
            _merge(deps, b.w)
        for b in writes:
            if not b.multi:
                _merge(deps, b.w)
            _merge(deps, b.r)
        return deps

    def _emit_waits(self, eng, deps, selfdep=False):
        me = id(self.esem[eng])
        w = self.waited[eng]
        for k, (s, v) in deps.items():
            if k == me and not selfdep:
                continue
            if w.get(k, 0) >= v:
                continue
            w[k] = v
            self.trace[eng].append(("wait", k, v))
            self.prog[eng].append(lambda e, s=s, v=v: e.wait_ge(s, v))

    def _finish(self, ev, reads, writes):
        for b in writes:
            if b.multi:
                _merge(b.w, ev)
            else:
                b.w = dict(ev)
            b.r = {}
        for b in reads:
            _merge(b.r, ev)
        _merge(self.allev, ev)

    def op(self, eng, fn, reads=(), writes=(), selfdep=False):
        deps = self._deps(eng, reads, writes)
        self._emit_waits(eng, deps, selfdep)
        self.ecount[eng] += 1
        s, v = self.esem[eng], self.ecount[eng]
        self.trace[eng].append(("inc", id(s), 1))
        self.prog[eng].append(lambda e, fn=fn, s=s: fn(e).then_inc(s, 1))
        self._finish({id(s): (s, v)}, reads, writes)

    def dma(self, q, out_ap, in_ap, slot, reads=(), writes=()):
        deps = self._deps(q, reads, writes)
        self._emit_waits(q, deps)
        if slot.sem is None:
            slot.sem, slot.dcount = self.newsem("d_" + slot.name)
        slot.dcount += 16
        s, v = slot.sem, slot.dcount
        self.trace[q].append(("inc", id(s), 16))
        self.prog[q].append(lambda e, o=out_ap, i=in_ap, s=s: e.dma_start(out=o, in_=i).then_inc(s, 16))
        self._finish({id(s): (s, v)}, reads, writes)

    def cc(self, in_ap, out_ap, slot, reads=(), writes=()):
        deps = self._deps("pool", reads, writes)
        self._emit_waits("pool", deps)
        if slot.sem is None:
            slot.sem, slot.dcount = self.newsem("c_" + slot.name)
        slot.dcount += 1
        s, v = slot.sem, slot.dcount
        self.trace["pool"].append(("inc", id(s), 1))
        self.prog["pool"].append(lambda e, o=out_ap, i=in_ap, s=s: e.collective_compute(
            "AllGather", ALU.bypass, replica_groups=PAIRS, ins=[i], outs=[o]).then_inc(s, 1))
        self._finish({id(s): (s, v)}, reads, writes)

    def barrier(self, engines=None):
        for e in (engines or self.ENG):
            self._emit_waits(e, self.allev)

    def emit(self):
        nc = self.nc
        with nc.Block() as block:
            @block.tensor
            def _(e):
                for f in self.prog["pe"]:
                    f(e)

            @block.scalar
            def _(e):
                for f in self.prog["act"]:
                    f(e)

            @block.vector
            def _(e):
                for f in self.prog["dve"]:
                    f(e)

            @block.gpsimd
            def _(e):
                for f in self.prog["pool"]:
                    f(e)

            @block.sync
            def _(e):
                for f in self.prog["sp"]:
                    f(e)


class Ctx:
    def __init__(self, nc, sch, stack):
        self.nc, self.sch, self.stack = nc, sch, stack
        self.psum = []
        self.pi = 0
        self.wi = 0
        self.ei = 0
        self.dbg = None
        self.dbgb = None
        self.cur = []

    def sb(self, stack, name, shape, dt):
        self.wi += 1
        name = f"sb{self.wi}_{name}"
        t = stack.enter_context(self.nc.sbuf_tensor(name, list(shape), dt))
        b = Buf(name, t)
        self.cur.append(b)
        return b

    def end_phase(self):
        self.sch.barrier()
        self.sch.release(self.cur)
        self.cur = []

    def bank(self):
        b = self.psum[self.pi % len(self.psum)]
        self.pi += 1
        return b

    def evac_eng(self):
        self.ei += 1
        return "act" if self.ei % 2 else "dve"


def load_weight_slab(cx, wslot, w_dram_view, kchunks):
    v = w_dram_view.rearrange("(k p) c -> p k c", p=128)
    step = 4
    for k0 in range(0, kchunks, step):
        cx.sch.dma("pool", wslot.ap[:, k0:k0 + step, :], v[:, k0:k0 + step, :], wslot, writes=[wslot])


def rms_scale_tile(cx, st, xt, sq, rstd, ones, nk, ncols, dim):
    sch = cx.sch
    bank = cx.bank()
    for k in range(nk):
        sch.op("act", lambda e, k=k: e.activation(sq.ap[:, k, :ncols], xt.ap[:, k, :ncols], AF.Square),
               reads=[xt], writes=[sq])
    def mm(e):
        ins = None
        for k in range(nk):
            ins = e.matmul(bank.ap[:, :ncols], ones.ap[:, :], sq.ap[:, k, :ncols], start=(k == 0), stop=(k == nk - 1))
        return ins
    sch.op("pe", mm, reads=[sq, ones], writes=[bank])
    sch.op("dve", lambda e: e.tensor_scalar(rstd.ap[:, :ncols], bank.ap[:, :ncols], 1.0 / dim, EPS, ALU.mult, ALU.add),
           reads=[bank], writes=[rstd])
    sch.op("act", lambda e: e.activation(rstd.ap[:, :ncols], rstd.ap[:, :ncols], AF.Sqrt), reads=[rstd], writes=[rstd])
    sch.op("dve", lambda e: e.reciprocal(rstd.ap[:, :ncols], rstd.ap[:, :ncols]), reads=[rstd], writes=[rstd])


def phase_norm_proj(cx, xT_d, xT_buf, g_col, w_view_fn, nslabs, slab_out, consts, post_load=None):
    nc, sch = cx.nc, cx.sch
    with contextlib.ExitStack() as st:
        hT = cx.sb(st, "hT", [128, 16, SL], BF16)
        xt = [cx.sb(st, f"xt{i}", [128, 16, 256], F32) for i in range(2)]
        sq = [cx.sb(st, f"sq{i}", [128, 16, 256], BF16) for i in range(2)]
        rstd = [cx.sb(st, f"rstd{i}", [128, 256], F32) for i in range(2)]
        wsl = [cx.sb(st, f"wsl{i}", [128, 16, 512], BF16) for i in range(2)]
        stg32 = [cx.sb(st, f"stg32_{i}", [128, 512], F32) for i in range(3)]
        stg16 = [cx.sb(st, f"stg16_{i}", [128, 512], BF16) for i in range(3)]
        tmp32 = [cx.sb(st, f"tmp32_{i}", [128, 512], F32) for i in range(2)]
        ones = consts["ones"]
        TT = 256
        xv = xT_d.rearrange("(k p) s -> p k s", p=128)
        for t in range(SL // TT):
            x_, sq_, r_ = xt[t % 2], sq[t % 2], rstd[t % 2]
            for k0 in range(0, 16, 8):
                sch.dma("sp", x_.ap[:, k0:k0 + 8, :], xv[:, k0:k0 + 8, t * TT:(t + 1) * TT], x_,
                        reads=[xT_buf], writes=[x_])
            rms_scale_tile(cx, st, x_, sq_, r_, ones, 16, TT, D)
            for k in range(16):
                sch.op("dve", lambda e, k=k, x_=x_, r_=r_, t=t: e.scalar_tensor_tensor(
                    hT.ap[:, k, t * TT:(t + 1) * TT], x_.ap[:, k, :], g_col.ap[:, k:k + 1], r_.ap[:, :],
                    ALU.mult, ALU.mult), reads=[x_, r_, g_col], writes=[hT], selfdep=(k == 0))
        si = 0
        for i in range(nslabs):
            w_ = wsl[i % 2]
            load_weight_slab(cx, w_, w_view_fn(i), 16)
            if post_load and i in post_load:
                post_load[i]()
            so = slab_out(i)
            groups = []
            if so["mode"] == "tok":
                for tb in range(NBL):
                    groups.append(("tok", tb, None))
            else:
                for cs in range(4):
                    for tt in range(SL // 512):
                        groups.append(("feat", cs, tt))
            for (mode, a, b) in groups:
                bank = cx.bank()
                if mode == "tok":
                    def mm(e, a=a, w_=w_, bank=bank):
                        ins = None
                        for k in range(16):
                            ins = e.matmul(bank.ap[:, :], hT.ap[:, k, a * 128:(a + 1) * 128], w_.ap[:, k, :],
                                           start=(k == 0), stop=(k == 15))
                        return ins
                    dst = so["dst"][a * 128:(a + 1) * 128, :]
                else:
                    def mm(e, a=a, b=b, w_=w_, bank=bank):
                        ins = None
                        for k in range(16):
                            ins = e.matmul(bank.ap[:, :], w_.ap[:, k, a * 128:(a + 1) * 128],
                                           hT.ap[:, k, b * 512:(b + 1) * 512], start=(k == 0), stop=(k == 15))
                        return ins
                    dst = so["dst"][a * 128:(a + 1) * 128, b * 512:(b + 1) * 512]
                sch.op("pe", mm, reads=[hT, w_], writes=[bank])
                stg = (stg32 if so["dt"] == F32 else stg16)[si % 3]
                si += 1
                if so.get("relu2"):
                    t32 = tmp32[si % 2]
                    sch.op("act", lambda e, bank=bank, t32=t32: e.activation(t32.ap[:, :], bank.ap[:, :], AF.Relu),
                           reads=[bank], writes=[t32])
                    sch.op("dve", lambda e, t32=t32, stg=stg: e.tensor_tensor(stg.ap[:, :], t32.ap[:, :], t32.ap[:, :], ALU.mult),
                           reads=[t32], writes=[stg])
                else:
                    eng = cx.evac_eng()
                    sc = so.get("scale", 1.0)
                    if eng == "act":
                        sch.op("act", lambda e, bank=bank, stg=stg, sc=sc: e.activation(
                            stg.ap[:, :], bank.ap[:, :], AF.Copy, scale=sc), reads=[bank], writes=[stg])
                    else:
                        sch.op("dve", lambda e, bank=bank, stg=stg, sc=sc: e.tensor_scalar(
                            stg.ap[:, :], bank.ap[:, :], sc, None, ALU.mult), reads=[bank], writes=[stg])
                sch.dma("sp", dst, stg.ap[:, :], stg, reads=[stg], writes=[so["dbuf"]])
        cx.end_phase()


def seq(sch, eng, items):
    for n, (fn, r, w) in enumerate(items):
        sch.op(eng, fn, reads=r, writes=w, selfdep=(n > 0))


def dbuf(name):
    b = Buf(name)
    b.multi = True
    return b


def phase_proj_norm_res(cx, inT_d, in_buf, nk, w_view_fn, g_col, x_src, x_src_buf, x_dst, x_dst_buf, consts, wc_d, wc_buf):
    sch = cx.sch
    nkq = nk // 16
    nslot = 2 if nk == 16 else 1
    with contextlib.ExitStack() as st:
        inT = [[cx.sb(st, f"inT{i}_{q}", [128, 16, 512], BF16) for q in range(nkq)] for i in range(nslot)]
        yT = cx.sb(st, "yT", [128, 16, 512], F32)
        xt = cx.sb(st, "xres", [128, 16, 512], F32)
        sq = cx.sb(st, "ysq", [128, 16, 512], BF16)
        rstd = cx.sb(st, "yrstd", [128, 512], F32)
        wsl = [cx.sb(st, f"w2sl{i}", [128, 16, 256], BF16) for i in range(3)]
        ones = consts["ones"]
        inv = inT_d.rearrange("(k p) s -> p k s", p=128)
        xsv = x_src.rearrange("(k p) s -> p k s", p=128)
        xdv = x_dst.rearrange("(k p) s -> p k s", p=128)
        wi = 0
        NT = SL // 512

        def load_in(tt):
            for q in range(nkq):
                b = inT[tt % nslot][q]
                for k0 in range(0, 16, 8):
                    sch.dma("sp", b.ap[:, k0:k0 + 8, :], inv[:, q * 16 + k0:q * 16 + k0 + 8, tt * 512:(tt + 1) * 512], b,
                            reads=[in_buf], writes=[b])

        def load_x(tt):
            for k0 in range(0, 16, 8):
                sch.dma("sp", xt.ap[:, k0:k0 + 8, :], xsv[:, k0:k0 + 8, tt * 512:(tt + 1) * 512], xt,
                        reads=[x_src_buf], writes=[xt])

        def epilogue(tt):
            rms_scale_tile(cx, st, yT, sq, rstd, ones, 16, 512, D)
            for k in range(16):
                sch.op("dve", lambda e, k=k: e.scalar_tensor_tensor(
                    yT.ap[:, k, :], yT.ap[:, k, :], g_col.ap[:, k:k + 1], rstd.ap[:, :], ALU.mult, ALU.mult),
                    reads=[yT, rstd, g_col], writes=[yT], selfdep=(k == 0))
            for k in range(16):
                sch.op("dve", lambda e, k=k: e.tensor_tensor(xt.ap[:, k, :], xt.ap[:, k, :], yT.ap[:, k, :], ALU.add),
                       reads=[yT, xt], writes=[xt])
            for k0 in range(0, 16, 8):
                sch.dma("sp", xdv[:, k0:k0 + 8, tt * 512:(tt + 1) * 512], xt.ap[:, k0:k0 + 8, :], xt,
                        reads=[xt], writes=[x_dst_buf])
            if tt + 1 < NT:
                load_x(tt + 1)

        load_in(0)
        load_x(0)
        for tt in range(NT):
            i_ = inT[tt % nslot]
            for cg in range(8):
                banks = [cx.bank(), cx.bank()]
                for kq in range(nkq):
                    w_ = wsl[wi % 3]
                    wi += 1
                    slab = cg * nkq + kq
                    if tt == 0:
                        wv = w_view_fn(kq * 2048, (kq + 1) * 2048, cg * 256, (cg + 1) * 256).rearrange("(k p) c -> p k c", p=128)
                        for k0 in range(0, 16, 8):
                            sch.dma("pool", w_.ap[:, k0:k0 + 8, :], wv[:, k0:k0 + 8, :], w_, writes=[w_])
                        sch.dma("sp", wc_d[slab], w_.ap[:, :, :], w_, reads=[w_], writes=[wc_buf])
                    else:
                        sch.dma("pool", w_.ap[:, :, :], wc_d[slab], w_, reads=[wc_buf], writes=[w_])
                    ib = i_[kq]
                    def mm(e, kq=kq, w_=w_, banks=banks, ib=ib):
                        ins = None
                        for cs in range(2):
                            for k in range(16):
                                ins = e.matmul(banks[cs].ap[:, :], w_.ap[:, k, cs * 128:(cs + 1) * 128],
                                               ib.ap[:, k, :], start=(kq == 0 and k == 0),
                                               stop=(kq == nkq - 1 and k == 15))
                        return ins
                    sch.op("pe", mm, reads=[ib, w_], writes=banks)
                if cg == 0 and tt > 0:
                    epilogue(tt - 1)
                for cs in range(2):
                    kk = cg * 2 + cs
                    if cs == 0:
                        sch.op("act", lambda e, b=banks[cs], kk=kk: e.activation(yT.ap[:, kk, :], b.ap[:, :], AF.Copy),
                               reads=[banks[cs]], writes=[yT])
                    else:
                        sch.op("dve", lambda e, b=banks[cs], kk=kk: e.tensor_copy(yT.ap[:, kk, :], b.ap[:, :]),
                               reads=[banks[cs]], writes=[yT])
            if tt + 1 < NT:
                load_in(tt + 1)
        epilogue(NT - 1)
        cx.end_phase()


def phase_conv(cx, pda, pdg, b_da, b_dg, hgat, b_hg, flag, cw, cb, cn, l, mixT, mix_buf, consts):
    sch = cx.sch
    with contextlib.ExitStack() as st:
        hb = [cx.sb(st, f"cv_h{i}", [128, 30 + SL], F32) for i in range(2)]
        gt = [cx.sb(st, f"cv_g{i}", [128, SL], F32) for i in range(2)]
        hg = [cx.sb(st, f"cv_hg{i}", [128, 32], F32) for i in range(2)]
        acc = [cx.sb(st, f"cv_acc{i}", [128, SL], F32) for i in range(4)]
        sq = cx.sb(st, "cv_sq", [128, 4, SL], BF16)
        rstd = cx.sb(st, "cv_rstd", [128, SL], F32)
        tmp = [cx.sb(st, f"cv_tmp{i}", [128, SL], F32) for i in range(2)]
        ob = [cx.sb(st, f"cv_ob{i}", [128, SL], BF16) for i in range(2)]
        ones = consts["ones"]
        for cc in range(4):
            h_, g_, hg_ = hb[cc % 2], gt[cc % 2], hg[cc % 2]
            eng = "dve"
            sch.dma("sp", h_.ap[:, 30:], pda[cc * 128:(cc + 1) * 128, :], h_, reads=[b_da], writes=[h_])
            sch.dma("sp", h_.ap[:, 0:30], hgat[cc * 128:(cc + 1) * 128, 2:32], h_, reads=[b_hg], writes=[h_])
            sch.dma("sp", hg_.ap[:, 0:30], hgat[GW + cc * 128:GW + (cc + 1) * 128, 2:32], hg_, reads=[b_hg], writes=[hg_])
            sch.dma("sp", g_.ap[:, :], pdg[cc * 128:(cc + 1) * 128, :], g_, reads=[b_dg], writes=[g_])
            sch.op("act", lambda e, g_=g_: e.activation(g_.ap[:, :], g_.ap[:, :], AF.Sigmoid), reads=[g_], writes=[g_])
            sch.op("act", lambda e, hg_=hg_: e.activation(hg_.ap[:, 0:30], hg_.ap[:, 0:30], AF.Sigmoid), reads=[hg_], writes=[hg_])
            sch.op(eng, lambda e, h_=h_, g_=g_: e.tensor_tensor(h_.ap[:, 30:], h_.ap[:, 30:], g_.ap[:, :], ALU.mult),
                   reads=[h_, g_], writes=[h_])
            sch.op(eng, lambda e, h_=h_, hg_=hg_: e.scalar_tensor_tensor(
                h_.ap[:, 0:30], h_.ap[:, 0:30], flag.ap[:, 0:1], hg_.ap[:, 0:30], ALU.mult, ALU.mult),
                reads=[h_, hg_, flag], writes=[h_])
            a_ = acc[cc]
            wbase = (l * 4 + cc) * CONVW
            bcol = cb.ap[:, l * 4 + cc:l * 4 + cc + 1]
            def conv(e, h_=h_, a_=a_, wbase=wbase, bcol=bcol):
                ins = e.tensor_scalar(a_.ap[:, :], h_.ap[:, 0:SL], cw.ap[:, wbase:wbase + 1], bcol, ALU.mult, ALU.add)
                for j in range(1, CONVW):
                    ins = e.scalar_tensor_tensor(a_.ap[:, :], h_.ap[:, j:j + SL], cw.ap[:, wbase + j:wbase + j + 1],
                                                 a_.ap[:, :], ALU.mult, ALU.add)
                return ins
            sch.op(eng, conv, reads=[h_, cw, cb], writes=[a_], selfdep=True)
            sch.op("act", lambda e, a_=a_, cc=cc: e.activation(sq.ap[:, cc, :], a_.ap[:, :], AF.Square),
                   reads=[a_], writes=[sq])
        for tt in range(SL // 512):
            bank = cx.bank()
            def mm(e, tt=tt, bank=bank):
                ins = None
                for cc in range(4):
                    ins = e.matmul(bank.ap[:, :], ones.ap[:, :], sq.ap[:, cc, tt * 512:(tt + 1) * 512],
                                   start=(cc == 0), stop=(cc == 3))
                return ins
            sch.op("pe", mm, reads=[sq, ones], writes=[bank])
            sch.op("dve", lambda e, tt=tt, bank=bank: e.tensor_scalar(
                rstd.ap[:, tt * 512:(tt + 1) * 512], bank.ap[:, :], 1.0 / GW, EPS, ALU.mult, ALU.add),
                reads=[bank], writes=[rstd])
        sch.op("act", lambda e: e.activation(rstd.ap[:, :], rstd.ap[:, :], AF.Sqrt), reads=[rstd], writes=[rstd])
        sch.op("dve", lambda e: e.reciprocal(rstd.ap[:, :], rstd.ap[:, :]), reads=[rstd], writes=[rstd])
        for cc in range(4):
            t_, o_ = tmp[cc % 2], ob[cc % 2]
            sch.op("dve", lambda e, cc=cc, t_=t_: e.scalar_tensor_tensor(
                t_.ap[:, :], acc[cc].ap[:, :], cn.ap[:, l * 4 + cc:l * 4 + cc + 1], rstd.ap[:, :], ALU.mult, ALU.mult),
                reads=[acc[cc], rstd, cn], writes=[t_], selfdep=True)
            sch.op("act", lambda e, t_=t_, o_=o_: e.activation(o_.ap[:, :], t_.ap[:, :], AF.Silu), reads=[t_], writes=[o_])
            sch.dma("sp", mixT[1536 + cc * 128:1536 + (cc + 1) * 128, :], o_.ap[:, :], o_, reads=[o_], writes=[mix_buf])
        cx.end_phase()


def phase_gmlp(cx, pu, pv, b_u, b_v, gw_d, gbT, l, mixT, mix_buf, consts):
    sch = cx.sch
    with contextlib.ExitStack() as st:
        wraw = cx.sb(st, "gm_wraw", [128, 8, 128], F32)
        WT = cx.sb(st, "gm_WT", [128, 8, 128], BF16)
        bfull = cx.sb(st, "gm_bfull", [128, GW], F32)
        ut = [cx.sb(st, f"gm_u{i}", [128, GW], F32) for i in range(2)]
        vt = [cx.sb(st, f"gm_v{i}", [128, GW], F32) for i in range(2)]
        vn = [cx.sb(st, f"gm_vn{i}", [128, GW], BF16) for i in range(2)]
        t32 = [cx.sb(st, f"gm_t{i}", [128, GW], F32) for i in range(2)]
        oa = [cx.sb(st, f"gm_oa{i}", [128, GW], BF16) for i in range(2)]
        oT = [cx.sb(st, f"gm_oT{i}", [128, 4, 128], BF16) for i in range(2)]
        st1 = [cx.sb(st, f"gm_s{i}", [128, 4], F32) for i in range(2)]
        identf, identb, tri = consts["identf"], consts["identb"], consts["tri"]
        pbf = consts["psum_bf"]
        sch.dma("sp", wraw.ap[:, :, :], gw_d[l], wraw, writes=[wraw])
        for g in range(8):
            sch.op("dve", lambda e, g=g: e.tensor_tensor(WT.ap[:, g, :], wraw.ap[:, g, :], tri.ap[:, :], ALU.mult),
                   reads=[wraw, tri], writes=[WT])
        sch.op("dve", lambda e: e.memset(bfull.ap[:, :], 0.0), writes=[bfull])
        for g in range(8):
            sch.op("dve", lambda e, g=g: e.tensor_scalar(
                bfull.ap[:, g * 64:(g + 1) * 64], bfull.ap[:, g * 64:(g + 1) * 64], gbT.ap[:, l * 8 + g:l * 8 + g + 1], 0.0, ALU.add, ALU.add),
                reads=[bfull, gbT], writes=[bfull], selfdep=(g == 0))
        for c in range(NBL):
            u_, v_, vn_, t_, oa_, oT_, s_ = ut[c % 2], vt[c % 2], vn[c % 2], t32[c % 2], oa[c % 2], oT[c % 2], st1[c % 2]
            sch.dma("sp", u_.ap[:, :], pu[c * 128:(c + 1) * 128, :], u_, reads=[b_u], writes=[u_])
            sch.dma("sp", v_.ap[:, :], pv[c * 128:(c + 1) * 128, :], v_, reads=[b_v], writes=[v_])
            sch.op("act", lambda e, u_=u_: e.activation(u_.ap[:, :], u_.ap[:, :], AF.Gelu_apprx_tanh), reads=[u_], writes=[u_])
            sch.op("act", lambda e, v_=v_: e.activation(v_.ap[:, :], v_.ap[:, :], AF.Gelu_apprx_tanh), reads=[v_], writes=[v_])
            def rs0(e, v_=v_, s_=s_):
                e.memset(s_.ap[:, 1:2], 0.0)
                return e.reduce_sum(s_.ap[:, 0:1], v_.ap[:, :], AX.X)
            sch.op("dve", rs0, reads=[v_], writes=[s_])
            sch.op("act", lambda e, s_=s_: e.activation(s_.ap[:, 0:1], s_.ap[:, 0:1], AF.Copy, scale=-1.0 / GW), reads=[s_], writes=[s_])
            sch.op("dve", lambda e, v_=v_, s_=s_: e.tensor_scalar(v_.ap[:, :], v_.ap[:, :], s_.ap[:, 0:1], 0.0, ALU.add, ALU.add),
                   reads=[v_, s_], writes=[v_])
            sch.op("act", lambda e, v_=v_, s_=s_, t_=t_: e.activation(t_.ap[:, :], v_.ap[:, :], AF.Square, accum_out=s_.ap[:, 1:2]),
                   reads=[v_], writes=[t_, s_])
            sch.op("dve", lambda e, s_=s_: e.tensor_scalar(s_.ap[:, 1:2], s_.ap[:, 1:2], 1.0 / GW, EPS, ALU.mult, ALU.add), reads=[s_], writes=[s_])
            sch.op("act", lambda e, s_=s_: e.activation(s_.ap[:, 1:2], s_.ap[:, 1:2], AF.Sqrt), reads=[s_], writes=[s_])
            sch.op("dve", lambda e, s_=s_: e.reciprocal(s_.ap[:, 2:3], s_.ap[:, 1:2]), reads=[s_], writes=[s_])
            sch.op("act", lambda e, v_=v_, s_=s_, vn_=vn_: e.activation(vn_.ap[:, :], v_.ap[:, :], AF.Copy, scale=s_.ap[:, 2:3]),
                   reads=[v_, s_], writes=[vn_])
            bank = cx.bank()
            def mm(e, vn_=vn_, bank=bank):
                ins = None
                for g in range(8):
                    ins = e.matmul(bank.ap[:, g * 64:(g + 1) * 64], WT.ap[:, g, :], vn_.ap[:, g * 64:(g + 1) * 64],
                                   start=True, stop=True)
                return ins
            sch.op("pe", mm, reads=[WT, vn_], writes=[bank])
            seq(sch, "dve", [
                (lambda e, bank=bank, t_=t_: e.tensor_tensor(t_.ap[:, :], bank.ap[:, :], bfull.ap[:, :], ALU.add), [bank, bfull], [t_]),
                (lambda e, t_=t_, u_=u_, oa_=oa_: e.tensor_tensor(oa_.ap[:, :], t_.ap[:, :], u_.ap[:, :], ALU.mult), [t_, u_], [oa_]),
            ])
            if c == 0 and cx.dbg is not None:
                for nm, b_ in [("u", u_), ("vn", vn_), ("t", t_), ("oa", oa_), ("bfull", bfull), ("v", v_)]:
                    sch.dma("sp", cx.dbg[nm], b_.ap[:, :], b_, reads=[b_], writes=[cx.dbgb])
            tb = cx.bank()
            def tr(e, oa_=oa_, tb=tb):
                ins = None
                for j in range(4):
                    ins = e.matmul(tb.ap[:, j * 128:(j + 1) * 128], oa_.ap[:, j * 128:(j + 1) * 128], identb.ap[:, :],
                                   start=True, stop=True)
                return ins
            sch.op("pe", tr, reads=[oa_, identb], writes=[tb])
            sch.op("act", lambda e, oT_=oT_, tb=tb: e.activation(oT_.ap[:, :, :], tb.ap[:, :].rearrange("p (j t) -> p j t", j=4), AF.Copy),
                   reads=[tb], writes=[oT_])
            sch.dma("sp", mixT[0:GW, c * 128:(c + 1) * 128].rearrange("(j p) t -> p j t", p=128), oT_.ap[:, :, :], oT_,
                    reads=[oT_], writes=[mix_buf])
        cx.end_phase()


def phase_attn(cx, kind, qT_d, kT_d, kprev_d, v_d, vprev_d, b_q, b_k, b_kg, b_v, b_vg, flag, l, mixT, mix_buf, mix_row0,
               consts, lam_bufs=None):
    sch = cx.sch
    diff = kind == "diff"
    E = 128 if diff else 64
    EA = E + 1
    alibi, ctab, tri, identb, pbf = consts["alibi"], consts["ctab"], consts["tri"], consts["identb"], consts["psum_bf"]
    DIFF_IDX = [0, 3, 6, 9]
    DIL_IDX = [1, 2, 4, 5, 7, 8, 10, 11]
    banks = cx.psum
    with contextlib.ExitStack() as st:
        KT = [cx.sb(st, f"at_K{i}", [128, S], BF16) for i in range(2)]
        QT = [cx.sb(st, f"at_Q{i}", [128, SL], BF16) for i in range(2)]
        nV = 1 if diff else 2
        VA = [[cx.sb(st, f"at_V{i}_{m}", [128, NB, EA], BF16) for m in range(nV)] for i in range(2)]
        PT = [cx.sb(st, f"at_P{i}", [128, 512], BF16) for i in range(4)]
        fin32 = [cx.sb(st, f"at_f{i}", [128, 2, 128], F32) for i in range(2)]
        fs = [cx.sb(st, f"at_s{i}", [128, 8], F32) for i in range(2)]
        ob = [cx.sb(st, f"at_ob{i}", [128, 128], BF16) for i in range(2)]
        oT = [cx.sb(st, f"at_oT{i}", [128, 512], BF16) for i in range(2)]
        for i in range(2):
            for m in range(nV):
                sch.op("dve", lambda e, i=i, m=m: e.memset(VA[i][m].ap[:, :, E:EA], 1.0), writes=[VA[i][m]])
        def acc_region(m, j):
            if diff:
                return m * 2 + j // 2, (j % 2) * EA
            return m, j * EA
        nacc = 4 if diff else 2
        accw = (2 if diff else 4) * EA
        asb = [cx.sb(st, f"at_acc{i}", [128, nacc, accw], F32) for i in range(2)]
        stb = [banks[4], banks[5], pbf]
        trb = banks[6]
        sti = 0
        pti = 0
        fi = 0
        for u in range(4):
            K_, Q_, V_ = KT[u % 2], QT[u % 2], VA[u % 2]
            sch.dma("sp", K_.ap[:, 0:SL], kprev_d[u * 128:(u + 1) * 128, :], K_, reads=[b_kg], writes=[K_])
            sch.dma("sp", K_.ap[:, SL:S], kT_d[u * 128:(u + 1) * 128, :], K_, reads=[b_k], writes=[K_])
            sch.dma("sp", Q_.ap[:, :], qT_d[u * 128:(u + 1) * 128, :], Q_, reads=[b_q], writes=[Q_])
            for m in range(nV):
                c0 = u * 128 + (0 if diff else m * 64)
                sch.dma("sp", V_[m].ap[:, 0:NBL, 0:E], vprev_d[:, c0:c0 + E].rearrange("(n p) e -> p n e", p=128), V_[m],
                        reads=[b_vg], writes=[V_[m]])
                sch.dma("sp", V_[m].ap[:, NBL:NB, 0:E], v_d[:, c0:c0 + E].rearrange("(n p) e -> p n e", p=128), V_[m],
                        reads=[b_v], writes=[V_[m]])
                sch.op("dve", lambda e, Vm=V_[m]: e.tensor_scalar(
                    Vm.ap[:, 0:NBL, :], Vm.ap[:, 0:NBL, :], flag.ap[:, 0:1], 0.0, ALU.mult, ALU.add),
                    reads=[V_[m], flag], writes=[V_[m]])
            for qg in (2, 3):
                steps = []
                for m in range(2):
                    hidx = DIFF_IDX[u] if diff else DIL_IDX[u * 2 + m]
                    slope = 2.0 ** (-8.0 * (hidx + 1) / N_ALIBI)
                    fine = slope > 0.26
                    for kb in range(qg * 4 + 4):
                        steps.append((m, hidx, fine, kb))
                pend = []
                started = set()
                def emit_pv(stp, P_):
                    m, hidx, fine, kb = stp
                    jlo = max(0, kb - qg * 4)
                    Vm = V_[0] if diff else V_[m]
                    for j in range(jlo, 4):
                        bi, c0 = acc_region(m, j)
                        first = bi not in started
                        started.add(bi)
                        ab = banks[bi]
                        sch.op("pe", lambda e, ab=ab, c0=c0, P_=P_, Vm=Vm, j=j, kb=kb, first=first, qg=qg: e.matmul(
                            ab.ap[:, c0:c0 + EA], P_.ap[:, j * 128:(j + 1) * 128], Vm.ap[:, kb, :], start=first,
                            stop=(kb == qg * 4 + j), skip_group_check=True),
                            reads=[P_, Vm], writes=[ab])
                for stp in steps:
                    m, hidx, fine, kb = stp
                    jlo = max(0, kb - qg * 4)
                    c_lo = jlo * 128
                    sb_ = stb[sti % 3]
                    sti += 1
                    P_ = PT[pti % 4]
                    pti += 1
                    sch.op("pe", lambda e, sb_=sb_, K_=K_, Q_=Q_, m=m, kb=kb, c_lo=c_lo, qg=qg: e.matmul(
                        sb_.ap[:, c_lo:512], K_.ap[m * 64:(m + 1) * 64, kb * 128:(kb + 1) * 128],
                        Q_.ap[m * 64:(m + 1) * 64, (qg - 2) * 512 + c_lo:(qg - 1) * 512], start=True, stop=True),
                        reads=[K_, Q_], writes=[sb_])
                    if fine:
                        def ex(e, sb_=sb_, P_=P_, hidx=hidx, kb=kb, jlo=jlo, qg=qg):
                            ins = None
                            for j in range(jlo, 4):
                                v = 16 + (kb - (qg * 4 + j) + 15)
                                ins = e.activation(P_.ap[:, j * 128:(j + 1) * 128], sb_.ap[:, j * 128:(j + 1) * 128], AF.Exp,
                                                   bias=alibi.ap[:, hidx * 32 + v:hidx * 32 + v + 1])
                            return ins
                    else:
                        def ex(e, sb_=sb_, P_=P_, hidx=hidx, kb=kb, c_lo=c_lo, qg=qg):
                            v = kb - 4 * qg - 2 + 14
                            return e.activation(P_.ap[:, c_lo:512], sb_.ap[:, c_lo:512], AF.Exp,
                                                bias=alibi.ap[:, hidx * 32 + v:hidx * 32 + v + 1])
                    sch.op("act", ex, reads=[sb_, alibi], writes=[P_])
                    if diff:
                        if kb >= qg * 4:
                            sch.op("dve", lambda e, P_=P_, jlo=jlo: e.tensor_tensor(
                                P_.ap[:, jlo * 128:(jlo + 1) * 128], P_.ap[:, jlo * 128:(jlo + 1) * 128], tri.ap[:, :], ALU.mult),
                                reads=[P_, tri], writes=[P_])
                    else:
                        u0 = qg * 512 + c_lo - kb * 128
                        sch.op("dve", lambda e, P_=P_, c_lo=c_lo, u0=u0: e.tensor_tensor(
                            P_.ap[:, c_lo:512], P_.ap[:, c_lo:512], ctab.ap[:, u0:u0 + 512 - c_lo], ALU.mult),
                            reads=[P_, ctab], writes=[P_])
                    pend.append((stp, P_))
                    if len(pend) > 3:
                        emit_pv(*pend.pop(0))
                while pend:
                    emit_pv(*pend.pop(0))
                asb_ = asb[(u * 4 + qg) % 2]
                for bi in range(nacc):
                    sch.op("dve", lambda e, bi=bi, asb_=asb_: e.tensor_copy(asb_.ap[:, bi, :], banks[bi].ap[:, 0:accw]),
                           reads=[banks[bi]], writes=[asb_])
                oT_ = oT[(u * 4 + qg) % 2]
                for j in range(4):
                    f_, s_, ob_ = fin32[fi % 2], fs[fi % 2], ob[fi % 2]
                    fi += 1
                    b0, c00 = acc_region(0, j)
                    b1, c01 = acc_region(1, j)
                    a0 = Buf("a0v", asb_.ap[:, b0, c00:c00 + EA])
                    a1 = Buf("a1v", asb_.ap[:, b1, c01:c01 + EA])
                    if diff:
                        neg_lam, gsub = lam_bufs
                        def recs(e, a0=a0, a1=a1, s_=s_):
                            e.memset(s_.ap[:, 2:3], 0.0)
                            e.reciprocal(s_.ap[:, 0:1], a0.ap[:, E:EA])
                            return e.reciprocal(s_.ap[:, 1:2], a1.ap[:, E:EA])
                        sch.op("dve", recs, reads=[asb_], writes=[s_], selfdep=True)
                        def nrm(e, a0=a0, a1=a1, s_=s_, f_=f_):
                            e.activation(f_.ap[:, 0, :], a0.ap[:, 0:E], AF.Copy, scale=s_.ap[:, 0:1])
                            return e.activation(f_.ap[:, 1, :], a1.ap[:, 0:E], AF.Copy, scale=s_.ap[:, 1:2])
                        sch.op("act", nrm, reads=[asb_, s_], writes=[f_])
                        sch.op("dve", lambda e, f_=f_: e.scalar_tensor_tensor(f_.ap[:, 0, :], f_.ap[:, 1, :], neg_lam.ap[:, 0:1], f_.ap[:, 0, :],
                                                                              ALU.mult, ALU.add), reads=[f_, neg_lam], writes=[f_])
                        sch.op("act", lambda e, f_=f_, s_=s_: e.activation(f_.ap[:, 1, :], f_.ap[:, 0, :], AF.Square, accum_out=s_.ap[:, 2:3]),
                               reads=[f_], writes=[f_, s_])
                        sch.op("dve", lambda e, s_=s_: e.tensor_scalar(s_.ap[:, 2:3], s_.ap[:, 2:3], 1.0 / 128, EPS, ALU.mult, ALU.add), reads=[s_], writes=[s_])
                        sch.op("act", lambda e, s_=s_: e.activation(s_.ap[:, 2:3], s_.ap[:, 2:3], AF.Sqrt), reads=[s_], writes=[s_])
                        sch.op("dve", lambda e, s_=s_: e.reciprocal(s_.ap[:, 3:4], s_.ap[:, 2:3]), reads=[s_], writes=[s_])
                        sch.op("act", lambda e, f_=f_, s_=s_: e.activation(f_.ap[:, 1, :], f_.ap[:, 0, :], AF.Copy, scale=s_.ap[:, 3:4]),
                               reads=[f_, s_], writes=[f_])
                        sch.op("dve", lambda e, f_=f_, ob_=ob_: e.tensor_tensor(ob_.ap[:, :], f_.ap[:, 1, :], gsub.ap[:, :], ALU.mult),
                               reads=[f_, gsub], writes=[ob_])
                    else:
                        def recs(e, a0=a0, a1=a1, s_=s_):
                            e.memset(s_.ap[:, 2:3], 0.0)
                            e.reciprocal(s_.ap[:, 0:1], a0.ap[:, E:EA])
                            return e.reciprocal(s_.ap[:, 1:2], a1.ap[:, E:EA])
                        sch.op("dve", recs, reads=[asb_], writes=[s_], selfdep=True)
                        def nrm(e, a0=a0, a1=a1, s_=s_, ob_=ob_):
                            e.activation(ob_.ap[:, 0:64], a0.ap[:, 0:E], AF.Copy, scale=s_.ap[:, 0:1])
                            return e.activation(ob_.ap[:, 64:128], a1.ap[:, 0:E], AF.Copy, scale=s_.ap[:, 1:2])
                        sch.op("act", nrm, reads=[asb_, s_], writes=[ob_])
                    sch.op("pe", lambda e, ob_=ob_, j=j: e.matmul(trb.ap[:, j * 128:(j + 1) * 128], ob_.ap[:, :], identb.ap[:, :],
                                                                  start=True, stop=True),
                           reads=[ob_, identb], writes=[trb])
                sch.op("act", lambda e, oT_=oT_: e.activation(oT_.ap[:, :], trb.ap[:, :], AF.Copy), reads=[trb], writes=[oT_])
                sch.dma("sp", mixT[mix_row0 + u * 128:mix_row0 + (u + 1) * 128, (qg - 2) * 512:(qg - 1) * 512], oT_.ap[:, :], oT_,
                        reads=[oT_], writes=[mix_buf])
        cx.end_phase()


def build(nlayers, first_layer=0, debug=False):
    nc = bass.Bass("TRN2", target_bir_lowering=False)
    skind = "ExternalOutput" if debug else "Internal"
    L = DEPTH
    ein = lambda n, shp, dt=F32: nc.dram_tensor(n, list(shp), dt, kind="ExternalInput").ap()
    scr = lambda n, shp, dt: nc.dram_tensor(n, list(shp), dt, kind=skind).ap()
    xT_in = ein("xT", [D, SL])
    flag_in = ein("flag", [128, 1])
    w_in = ein("w_in", [L, D, INW])
    w_out = ein("w_out", [L, D, D])
    w_ff1 = ein("w_ff1", [L, D, DFF])
    w_ff2 = ein("w_ff2", [L, DFF, D])
    gmlp_w = ein("gmlp_w", [L, 128, 8, 128])
    gcols = ein("gcols", [128, 4 * L * 16])
    c_f32 = ein("c_f32", [128, 384 + 128])
    c_bf = ein("c_bf", [128, 128 + 128 + 128 + S])
    p_cols = ein("p_cols", [128, L * 4 * CONVW + L * 4 + L * 4 + L * 8])
    p_rep = ein("p_rep", [128, L * 256 + L * 128])
    outT = nc.dram_tensor("outT", [D, SL], F32, kind="ExternalOutput").ap()
    pu, pv = scr("s_u", [SL, GW], F32), scr("s_v", [SL, GW], F32)
    pbq, pcq = scr("s_bq", [GW, SL], BF16), scr("s_cq", [GW, SL], BF16)
    ks_t = nc.dram_tensor("s_ksend", [2 * GW, SL], BF16, kind="Internal")
    kg_t = nc.dram_tensor("s_kgat", [4 * GW, SL], BF16, kind="Internal")
    vs_t = nc.dram_tensor("s_vsend", [2 * SL, GW], BF16, kind="Internal")
    vg_t = nc.dram_tensor("s_vgat", [4 * SL, GW], BF16, kind="Internal")
    hs_t = nc.dram_tensor("s_hsend", [2 * GW, 32], F32, kind="Internal")
    hg_t = nc.dram_tensor("s_hgat", [4 * GW, 32], F32, kind="Internal")
    ksend, kgat, vsend, vgat, hsend, hgat = (t.ap() for t in (ks_t, kg_t, vs_t, vg_t, hs_t, hg_t))
    pbk, pck = ksend[0:GW, :], ksend[GW:2 * GW, :]
    pbv, pcv = vsend[0:SL, :], vsend[SL:2 * SL, :]
    pda, pdg = scr("s_da", [GW, SL], F32), scr("s_dg", [GW, SL], F32)
    mixT = scr("s_mixT", [D, SL], BF16)
    fT = scr("s_fT", [DFF, SL], BF16)
    xA, xB = scr("s_xA", [D, SL], F32), scr("s_xB", [D, SL], F32)
    wcache = scr("s_wc", [32, 128, 16, 256], BF16)

    with contextlib.ExitStack() as stack:
        sch = Sched(nc, stack)
        cx = Ctx(nc, sch, stack)
        for i in range(7):
            t = stack.enter_context(nc.psum_tensor(f"ps{i}", [128, 512], F32))
            cx.psum.append(Buf(f"ps{i}", t))
        if debug:
            cx.dbg = {n: nc.dram_tensor("dbg_" + n, shp, dt, kind="ExternalOutput").ap() for n, shp, dt in [
                ("u", [128, 512], F32), ("vn", [128, 512], BF16), ("t", [128, 512], F32), ("oa", [128, 512], BF16),
                ("bfull", [128, 512], F32), ("v", [128, 512], F32), ("WT", [128, 8, 128], BF16)]}
            cx.dbgb = dbuf("dbgb")
        consts = {}
        pbf_t = stack.enter_context(nc.psum_tensor("psbf", [128, 512], F32))
        consts["psum_bf"] = Buf("psbf", pbf_t)
        cf = cx.sb(stack, "c_f32", [128, 512], F32)
        cb_ = cx.sb(stack, "c_bf", [128, 384 + S], BF16)
        pc = cx.sb(stack, "p_cols", [128, L * 4 * CONVW + L * 16], F32)
        pr = cx.sb(stack, "p_rep", [128, L * 384], F32)
        gc = cx.sb(stack, "gc", [128, 4 * L * 16], F32)
        flag = cx.sb(stack, "flag", [128, 1], F32)
        ccs = [cx.sb(stack, f"ccslot{i}", [128, 1], F32) for i in range(4)]
        sch.dma("sp", cf.ap[:, :], c_f32[:, :], cf, writes=[cf])
        sch.dma("pool", cb_.ap[:, :], c_bf[:, :], cb_, writes=[cb_])
        sch.dma("sp", pc.ap[:, :], p_cols[:, :], pc, writes=[pc])
        sch.dma("sp", pr.ap[:, :], p_rep[:, :], pr, writes=[pr])
        sch.dma("sp", gc.ap[:, :], gcols[:, :], gc, writes=[gc])
        sch.dma("sp", flag.ap[:, :], flag_in[:, :], flag, writes=[flag])
        def view(parent, ap):
            b = Buf(parent.name + "_v", ap)
            b.w = parent.w
            return b
        consts["alibi"] = view(cf, cf.ap[:, 0:384])
        consts["identf"] = view(cf, cf.ap[:, 384:512])
        consts["ones"] = view(cb_, cb_.ap[:, 0:128])
        consts["identb"] = view(cb_, cb_.ap[:, 128:256])
        consts["tri"] = view(cb_, cb_.ap[:, 256:384])
        consts["ctab"] = view(cb_, cb_.ap[:, 384:384 + S])
        o1 = L * 4 * CONVW
        cw = view(pc, pc.ap[:, 0:o1])
        cbias = view(pc, pc.ap[:, o1:o1 + L * 4])
        cnorm = view(pc, pc.ap[:, o1 + L * 4:o1 + L * 8])
        gbT = view(pc, pc.ap[:, o1 + L * 8:o1 + L * 16])
        lamw = cx.sb(stack, "lamw", [128, 16], F32)
        lamt = cx.sb(stack, "lamt", [128, 128], F32)
        lamt2 = cx.sb(stack, "lamt2", [128, 128], F32)
        gsub = cx.sb(stack, "gsub", [128, 128], F32)
        neg_lam = cx.sb(stack, "neg_lam", [128, 1], F32)
        cx.cur = []

        bx = {n: dbuf("b_" + n) for n in ["xin", "xA", "xB", "out", "u", "v", "bq", "bk", "bv", "cq", "ck", "cv", "da", "dg", "mix", "f", "wc", "kg", "vg", "hs", "hg"]}
        x_cur, x_cur_b = xT_in, bx["xin"]
        for li in range(nlayers):
            l = first_layer + li
            last = li == nlayers - 1
            g = lambda which: view(gc, gc.ap[:, (which * L + l) * 16:(which * L + l) * 16 + 16])
            outs = [
                dict(mode="tok", dst=pu, dt=F32, dbuf=bx["u"]),
                dict(mode="tok", dst=pv, dt=F32, dbuf=bx["v"]),
                dict(mode="feat", dst=pbq, dt=BF16, scale=0.125, dbuf=bx["bq"]),
                dict(mode="feat", dst=pbk, dt=BF16, dbuf=bx["bk"]),
                dict(mode="tok", dst=pbv, dt=BF16, dbuf=bx["bv"]),
                dict(mode="feat", dst=pcq, dt=BF16, scale=0.125, dbuf=bx["cq"]),
                dict(mode="feat", dst=pck, dt=BF16, dbuf=bx["ck"]),
                dict(mode="tok", dst=pcv, dt=BF16, dbuf=bx["cv"]),
                dict(mode="feat", dst=pda, dt=F32, dbuf=bx["da"]),
                dict(mode="feat", dst=pdg, dt=F32, dbuf=bx["dg"]),
            ]
            order = [3, 4, 6, 7, 8, 9, 0, 1, 2, 5]
            def gather_kv():
                sch.cc(ks_t.ap().opt(), kg_t.ap().opt(), ccs[0], reads=[bx["bk"], bx["ck"]], writes=[bx["kg"]])
                sch.cc(vs_t.ap().opt(), vg_t.ap().opt(), ccs[1], reads=[bx["bv"], bx["cv"]], writes=[bx["vg"]])
            def gather_halo():
                sch.dma("sp", hsend[0:GW, :], pda[:, SL - 32:SL], ccs[3], reads=[bx["da"]], writes=[bx["hs"]])
                sch.dma("sp", hsend[GW:2 * GW, :], pdg[:, SL - 32:SL], ccs[3], reads=[bx["dg"]], writes=[bx["hs"]])
                sch.cc(hs_t.ap().opt(), hg_t.ap().opt(), ccs[2], reads=[bx["hs"]], writes=[bx["hg"]])
            phase_norm_proj(cx, x_cur, x_cur_b, g(0), lambda i, l=l: w_in[l, :, order[i] * 512:(order[i] + 1) * 512], 10,
                            lambda i: outs[order[i]], consts, post_load={6: gather_kv, 8: gather_halo})
            lam_init = 0.8 - 0.6 * math.exp(-0.3 * l)
            lp = pr.ap[:, l * 256:(l + 1) * 256]
            def lam1(e, lp=lp):
                e.memset(lamw.ap[:, 0:2], 0.0)
                e.tensor_tensor(lamt.ap[:, 0:64], lp[:, 0:64], lp[:, 64:128], ALU.mult)
                return e.tensor_tensor(lamt.ap[:, 64:128], lp[:, 128:192], lp[:, 192:256], ALU.mult)
            sch.op("dve", lam1, reads=[pr], writes=[lamt, lamw])
            def lam2(e):
                e.activation(lamt2.ap[:, 0:64], lamt.ap[:, 0:64], AF.Copy, accum_out=lamw.ap[:, 0:1])
                return e.activation(lamt2.ap[:, 64:128], lamt.ap[:, 64:128], AF.Copy, accum_out=lamw.ap[:, 1:2])
            sch.op("act", lam2, reads=[lamt], writes=[lamt2, lamw])
            sch.op("dve", lambda e: e.tensor_copy(lamw.ap[:, 2:4], lamw.ap[:, 0:2]), reads=[lamw], writes=[lamw])
            sch.op("act", lambda e: e.activation(lamw.ap[:, 4:6], lamw.ap[:, 2:4], AF.Exp), reads=[lamw], writes=[lamw])
            sch.op("dve", lambda e, lam_init=lam_init: e.scalar_tensor_tensor(
                neg_lam.ap[:, 0:1], lamw.ap[:, 5:6], -lam_init, lamw.ap[:, 4:5], ALU.add, ALU.subtract), reads=[lamw], writes=[neg_lam])
            sch.op("dve", lambda e, l=l, lam_init=lam_init: e.tensor_scalar(
                gsub.ap[:, :], pr.ap[:, L * 256 + l * 128:L * 256 + (l + 1) * 128], 1.0 - lam_init, None, ALU.mult), reads=[pr], writes=[gsub])
            phase_gmlp(cx, pu, pv, bx["u"], bx["v"], gmlp_w, gbT, l, mixT, bx["mix"], consts)
            phase_attn(cx, "diff", pbq, pbk, kgat[0:GW, :], pbv, vgat[0:SL, :], bx["bq"], bx["bk"], bx["kg"], bx["bv"], bx["vg"],
                       flag, l, mixT, bx["mix"], 512, consts, lam_bufs=(neg_lam, gsub))
            phase_attn(cx, "dil", pcq, pck, kgat[GW:2 * GW, :], pcv, vgat[SL:2 * SL, :], bx["cq"], bx["ck"], bx["kg"], bx["cv"],
                       bx["vg"], flag, l, mixT, bx["mix"], 1024, consts)
            phase_conv(cx, pda, pdg, bx["da"], bx["dg"], hgat, bx["hg"], flag, cw, cbias, cnorm, l, mixT, bx["mix"], consts)
            phase_proj_norm_res(cx, mixT, bx["mix"], 16, lambda r0, r1, c0, c1, l=l: w_out[l, r0:r1, c0:c1], g(1),
                                x_cur, x_cur_b, xA, bx["xA"], consts, wcache, bx["wc"])
            fouts = [dict(mode="feat", dst=fT[i * 512:(i + 1) * 512, :], dt=BF16, relu2=True, dbuf=bx["f"]) for i in range(16)]
            phase_norm_proj(cx, xA, bx["xA"], g(2), lambda i, l=l: w_ff1[l, :, i * 512:(i + 1) * 512], 16,
                            lambda i: fouts[i], consts)
            xo, xob = (outT, bx["out"]) if last else (xB, bx["xB"])
            phase_proj_norm_res(cx, fT, bx["f"], 64, lambda r0, r1, c0, c1, l=l: w_ff2[l, r0:r1, c0:c1], g(3),
                                xA, bx["xA"], xo, xob, consts, wcache, bx["wc"])
            x_cur, x_cur_b = xB, bx["xB"]
        sch.barrier()
        sch.emit()
    return nc


def _count(d):
    return (d <= 128) * 1.0 + ((d % 4 == 0) & (d <= 512)) * 1.0 + ((d % 16 == 0) & (d <= 2048)) * 1.0


def host_consts():
    ki = np.arange(128, dtype=np.float64)[:, None]
    slopes = 2.0 ** (-8.0 * np.arange(1, N_ALIBI + 1, dtype=np.float64) / N_ALIBI)
    off = np.zeros(32)
    off[:16] = 128.0 * (np.arange(16) - 14)
    off[16:] = 128.0 * (np.arange(16) - 15) - 64.0
    alibi = (slopes[None, :, None] * (ki[:, :, None] + off[None, None, :])).reshape(128, 384)
    c_f32 = np.concatenate([alibi, np.eye(128)], axis=1).astype(np.float32)
    uu = np.arange(S)[None, :]
    dd = uu - np.arange(128)[:, None]
    ctab = np.where(dd >= 0, _count(np.maximum(dd, 0)), 0.0)
    tri = (np.arange(128)[None, :] >= np.arange(128)[:, None]) * 1.0
    c_bf = np.concatenate([np.ones((128, 128)), np.eye(128), tri, ctab], axis=1).astype(np.float32)
    return c_f32, c_bf


def host_params(p):
    L = DEPTH
    gc = np.stack([p["g_mix_pre"], p["g_mix_post"], p["g_ffn_pre"], p["g_ffn_post"]], 0)
    gcols = np.ascontiguousarray(gc.reshape(4, L, 16, 128).transpose(3, 0, 1, 2).reshape(128, 4 * L * 16))
    cw = p["conv_w"].reshape(L, CONVW, 4, 128).transpose(3, 0, 2, 1).reshape(128, L * 4 * CONVW)
    cb = p["conv_b"].reshape(L, 4, 128).transpose(2, 0, 1).reshape(128, L * 4)
    cn = p["conv_norm"].reshape(L, 4, 128).transpose(2, 0, 1).reshape(128, L * 4)
    gb = p["gmlp_b"].transpose(2, 0, 1).reshape(128, L * 8)
    p_cols = np.ascontiguousarray(np.concatenate([cw, cb, cn, gb], axis=1).astype(np.float32))
    lam = np.broadcast_to(p["diff_lam"].reshape(1, L * 256), (128, L * 256))
    sub = np.broadcast_to(p["diff_subln"].reshape(1, L * 128), (128, L * 128))
    p_rep = np.ascontiguousarray(np.concatenate([lam, sub], axis=1).astype(np.float32))
    return gcols, p_cols, p_rep


def make_in_maps(inputs, x_per_core):
    p = {k: np.asarray(v, dtype=np.float32) for k, v in inputs.items()}
    c_f32, c_bf = host_consts()
    gcols, p_cols, p_rep = host_params(p)
    maps = []
    for xc in x_per_core:
        maps.append({
            "flag": np.full((128, 1), float(len(maps) % 2), dtype=np.float32),
            "xT": np.ascontiguousarray(xc.T), "w_in": p["w_in"], "w_out": p["w_out"], "w_ff1": p["w_ff1"],
            "w_ff2": p["w_ff2"], "gmlp_w": np.ascontiguousarray(p["gmlp_w"].transpose(0, 3, 1, 2)), "gcols": gcols, "c_f32": c_f32, "c_bf": c_bf,
            "p_cols": p_cols, "p_rep": p_rep,
        })
    return maps


N_LAUNCH_LAYERS = 2


def kernel(**inputs):
    x = np.asarray(inputs["x"], dtype=np.float32)
    B = x.shape[0]
    cur = [x[c // 2, (c % 2) * SL:(c % 2 + 1) * SL] for c in range(8)]
    for l0 in range(0, DEPTH, N_LAUNCH_LAYERS):
        nc = build(N_LAUNCH_LAYERS, first_layer=l0)
        maps = make_in_maps(inputs, cur)
        res = run_bass_kernel_spmd(nc, maps, core_ids=list(range(8))).results
        cur = [np.ascontiguousarray(r["outT"].T) for r in res]
    out = np.stack([np.concatenate([cur[2 * b], cur[2 * b + 1]], axis=0) for b in range(B)], 0)
    return out.astype(np.float32)
```

```python
import contextlib
import math
import numpy as np
import concourse.bass as bass
import concourse.mybir as mybir
from concourse.bass_utils import run_bass_kernel_spmd

F32 = mybir.dt.float32
BF16 = mybir.dt.bfloat16
AF = mybir.ActivationFunctionType
ALU = mybir.AluOpType
AX = mybir.AxisListType

D = 2048
S = 2048
NB = S // 128
SL = S // 2
NBL = SL // 128
PAIRS = [[0, 1], [2, 3], [4, 5], [6, 7]]
DEPTH = 2
GW = 512
INW = 10 * GW
DFF = 4 * D
EPS = 1e-6
N_ALIBI = 12
CONVW = 31


class Buf:
    def __init__(self, name, ap=None, sem=None):
        self.name = name
        self.ap = ap
        self.sem = sem
        self.dcount = 0
        self.w = {}
        self.r = {}
        self.multi = False

    def __getitem__(self, k):
        return self.ap[k]


def _merge(dst, evs):
    for k, (s, v) in evs.items():
        if k not in dst or dst[k][1] < v:
            dst[k] = (s, v)


class Sched:
    ENG = ("pe", "act", "dve", "pool", "sp")

    def __init__(self, nc, stack):
        self.nc = nc
        self.stack = stack
        self.prog = {e: [] for e in self.ENG}
        self.esem = {e: stack.enter_context(nc.semaphore("es_" + e)) for e in self.ENG}
        self.ecount = {e: 0 for e in self.ENG}
        self.waited = {e: {} for e in self.ENG}
        self.allev = {}
        self.nsem = 5
        self.trace = {e: [] for e in self.ENG}
        self.free_sems = []

    def newsem(self, name):
        if self.free_sems:
            return self.free_sems.pop()
        self.nsem += 1
        return self.stack.enter_context(self.nc.semaphore(name)), 0

    def release(self, bufs):
        for b in bufs:
            if b.sem is not None:
                self.free_sems.append((b.sem, b.dcount))
                b.sem = None

    def _deps(self, eng, reads, writes):
        deps = {}
        for b in reads:
            _merge(deps, b.w)
        for b in writes:
            if not b.multi:
                _merge(deps, b.w)
            _merge(deps, b.r)
        return deps

    def _emit_waits(self, eng, deps, selfdep=False):
        me = id(self.esem[eng])
        w = self.waited[eng]
        for k, (s, v) in deps.items():
            if k == me and not selfdep:
                continue
            if w.get(k, 0) >= v:
                continue
            w[k] = v
            self.trace[eng].append(("wait", k, v))
            self.prog[eng].append(lambda e, s=s, v=v: e.wait_ge(s, v))

    def _finish(self, ev, reads, writes):
        for b in writes:
            if b.multi:
                _merge(b.w, ev)
            else:
                b.w = dict(ev)
            b.r = {}
        for b in reads:
            _merge(b.r, ev)
        _merge(self.allev, ev)

    def op(self, eng, fn, reads=(), writes=(), selfdep=False):
        deps = self._deps(eng, reads, writes)
        self._emit_waits(eng, deps, selfdep)
        self.ecount[eng] += 1
        s, v = self.esem[eng], self.ecount[eng]
        self.trace[eng].append(("inc", id(s), 1))
        self.prog[eng].append(lambda e, fn=fn, s=s: fn(e).then_inc(s, 1))
        self._finish({id(s): (s, v)}, reads, writes)

    def dma(self, q, out_ap, in_ap, slot, reads=(), writes=()):
        deps = self._deps(q, reads, writes)
        self._emit_waits(q, deps)
        if slot.sem is None:
            slot.sem, slot.dcount = self.newsem("d_" + slot.name)
        slot.dcount += 16
        s, v = slot.sem, slot.dcount
        self.trace[q].append(("inc", id(s), 16))
        self.prog[q].append(lambda e, o=out_ap, i=in_ap, s=s: e.dma_start(out=o, in_=i).then_inc(s, 16))
        self._finish({id(s): (s, v)}, reads, writes)

    def cc(self, in_ap, out_ap, slot, reads=(), writes=()):
        deps = self._deps("pool", reads, writes)
        self._emit_waits("pool", deps)
        if slot.sem is None:
            slot.sem, slot.dcount = self.newsem("c_" + slot.name)
        slot.dcount += 1
        s, v = slot.sem, slot.dcount
        self.trace["pool"].append(("inc", id(s), 1))
        self.prog["pool"].append(lambda e, o=out_ap, i=in_ap, s=s: e.collective_compute(
            "AllGather", ALU.bypass, replica_groups=PAIRS, ins=[i], outs=[o]).then_inc(s, 1))
        self._finish({id(s): (s, v)}, reads, writes)

    def barrier(self, engines=None):
        for e in (engines or self.ENG):
            self._emit_waits(e, self.allev)

    def emit(self):
        nc = self.nc
        with nc.Block() as block:
            @block.tensor
            def _(e):
                for f in self.prog["pe"]:
                    f(e)

            @block.scalar
            def _(e):
                for f in self.prog["act"]:
                    f(e)

            @block.vector
            def _(e):
                for f in self.prog["dve"]:
                    f(e)

            @block.gpsimd
            def _(e):
                for f in self.prog["pool"]:
                    f(e)

            @block.sync
            def _(e):
                for f in self.prog["sp"]:
                    f(e)


class Ctx:
    def __init__(self, nc, sch, stack):
        self.nc, self.sch, self.stack = nc, sch, stack
        self.psum = []
        self.pi = 0
        self.wi = 0
        self.ei = 0
        self.dbg = None
        self.dbgb = None
        self.cur = []

    def sb(self, stack, name, shape, dt):
        self.wi += 1
        name = f"sb{self.wi}_{name}"
        t = stack.enter_context(self.nc.sbuf_tensor(name, list(shape), dt))
        b = Buf(name, t)
        self.cur.append(b)
        return b

    def end_phase(self):
        self.sch.barrier()
        self.sch.release(self.cur)
        self.cur = []

    def bank(self):
        b = self.psum[self.pi % len(self.psum)]
        self.pi += 1
        return b

    def evac_eng(self):
        self.ei += 1
        return "act" if self.ei % 2 else "dve"


def load_weight_slab(cx, wslot, w_dram_view, kchunks):
    v = w_dram_view.rearrange("(k p) c -> p k c", p=128)
    step = 4
    for k0 in range(0, kchunks, step):
        cx.sch.dma("pool", wslot.ap[:, k0:k0 + step, :], v[:, k0:k0 + step, :], wslot, writes=[wslot])


def rms_scale_tile(cx, st, xt, sq, rstd, ones, nk, ncols, dim):
    sch = cx.sch
    bank = cx.bank()
    for k in range(nk):
        sch.op("act", lambda e, k=k: e.activation(sq.ap[:, k, :ncols], xt.ap[:, k, :ncols], AF.Square),
               reads=[xt], writes=[sq])
    def mm(e):
        ins = None
        for k in range(nk):
            ins = e.matmul(bank.ap[:, :ncols], ones.ap[:, :], sq.ap[:, k, :ncols], start=(k == 0), stop=(k == nk - 1))
        return ins
    sch.op("pe", mm, reads=[sq, ones], writes=[bank])
    sch.op("dve", lambda e: e.tensor_scalar(rstd.ap[:, :ncols], bank.ap[:, :ncols], 1.0 / dim, EPS, ALU.mult, ALU.add),
           reads=[bank], writes=[rstd])
    sch.op("act", lambda e: e.activation(rstd.ap[:, :ncols], rstd.ap[:, :ncols], AF.Sqrt), reads=[rstd], writes=[rstd])
    sch.op("dve", lambda e: e.reciprocal(rstd.ap[:, :ncols], rstd.ap[:, :ncols]), reads=[rstd], writes=[rstd])


def phase_norm_proj(cx, xT_d, xT_buf, g_col, w_view_fn, nslabs, slab_out, consts, post_load=None):
    nc, sch = cx.nc, cx.sch
    with contextlib.ExitStack() as st:
        hT = cx.sb(st, "hT", [128, 16, SL], BF16)
        xt = [cx.sb(st, f"xt{i}", [128, 16, 256], F32) for i in range(2)]
        sq = [cx.sb(st, f"sq{i}", [128, 16, 256], BF16) for i in range(2)]
        rstd = [cx.sb(st, f"rstd{i}", [128, 256], F32) for i in range(2)]
        wsl = [cx.sb(st, f"wsl{i}", [128, 16, 512], BF16) for i in range(2)]
        stg32 = [cx.sb(st, f"stg32_{i}", [128, 512], F32) for i in range(3)]
        stg16 = [cx.sb(st, f"stg16_{i}", [128, 512], BF16) for i in range(3)]
        tmp32 = [cx.sb(st, f"tmp32_{i}", [128, 512], F32) for i in range(2)]
        ones = consts["ones"]
        TT = 256
        xv = xT_d.rearrange("(k p) s -> p k s", p=128)
        for t in range(SL // TT):
            x_, sq_, r_ = xt[t % 2], sq[t % 2], rstd[t % 2]
            for k0 in range(0, 16, 8):
                sch.dma("sp", x_.ap[:, k0:k0 + 8, :], xv[:, k0:k0 + 8, t * TT:(t + 1) * TT], x_,
                        reads=[xT_buf], writes=[x_])
            rms_scale_tile(cx, st, x_, sq_, r_, ones, 16, TT, D)
            for k in range(16):
                sch.op("dve", lambda e, k=k, x_=x_, r_=r_, t=t: e.scalar_tensor_tensor(
                    hT.ap[:, k, t * TT:(t + 1) * TT], x_.ap[:, k, :], g_col.ap[:, k:k + 1], r_.ap[:, :],
                    ALU.mult, ALU.mult), reads=[x_, r_, g_col], writes=[hT], selfdep=(k == 0))
        si = 0
        for i in range(nslabs):
            w_ = wsl[i % 2]
            load_weight_slab(cx, w_, w_view_fn(i), 16)
            if post_load and i in post_load:
                post_load[i]()
            so = slab_out(i)
            groups = []
            if so["mode"] == "tok":
                for tb in range(NBL):
                    groups.append(("tok", tb, None))
            else:
                for cs in range(4):
                    for tt in range(SL // 512):
                        groups.append(("feat", cs, tt))
            for (mode, a, b) in groups:
                bank = cx.bank()
                if mode == "tok":
                    def mm(e, a=a, w_=w_, bank=bank):
                        ins = None
                        for k in range(16):
                            ins = e.matmul(bank.ap[:, :], hT.ap[:, k, a * 128:(a + 1) * 128], w_.ap[:, k, :],
                                           start=(k == 0), stop=(k == 15))
                        return ins
                    dst = so["dst"][a * 128:(a + 1) * 128, :]
                else:
                    def mm(e, a=a, b=b, w_=w_, bank=bank):
                        ins = None
                        for k in range(16):
                            ins = e.matmul(bank.ap[:, :], w_.ap[:, k, a * 128:(a + 1) * 128],
                                           hT.ap[:, k, b * 512:(b + 1) * 512], start=(k == 0), stop=(k == 15))
                        return ins
                    dst = so["dst"][a * 128:(a + 1) * 128, b * 512:(b + 1) * 512]
                sch.op("pe", mm, reads=[hT, w_], writes=[bank])
                stg = (stg32 if so["dt"] == F32 else stg16)[si % 3]
                si += 1
                if so.get("relu2"):
                    t32 = tmp32[si % 2]
                    sch.op("act", lambda e, bank=bank, t32=t32: e.activation(t32.ap[:, :], bank.ap[:, :], AF.Relu),
                           reads=[bank], writes=[t32])
                    sch.op("dve", lambda e, t32=t32, stg=stg: e.tensor_tensor(stg.ap[:, :], t32.ap[:, :], t32.ap[:, :], ALU.mult),
                           reads=[t32], writes=[stg])
                else:
                    eng = cx.evac_eng()
                    sc = so.get("scale", 1.0)
                    if eng == "act":
                        sch.op("act", lambda e, bank=bank, stg=stg, sc=sc: e.activation(
                            stg.ap[:, :], bank.ap[:, :], AF.Copy, scale=sc), reads=[bank], writes=[stg])
                    else:
                        sch.op("dve", lambda e, bank=bank, stg=stg, sc=sc: e.tensor_scalar(
                            stg.ap[:, :], bank.ap[:, :], sc, None, ALU.mult), reads=[bank], writes=[stg])
                sch.dma("sp", dst, stg.ap[:, :], stg, reads=[stg], writes=[so["dbuf"]])
        cx.end_phase()


def seq(sch, eng, items):
    for n, (fn, r, w) in enumerate(items):
        sch.op(eng, fn, reads=r, writes=w, selfdep=(n > 0))


def dbuf(name):
    b = Buf(name)
    b.multi = True
    return b


def phase_proj_norm_res(cx, inT_d, in_buf, nk, w_view_fn, g_col, x_src, x_src_buf, x_dst, x_dst_buf, consts, wc_d, wc_buf):
    sch = cx.sch
    nkq = nk // 16
    nslot = 2 if nk == 16 else 1
    with contextlib.ExitStack() as st:
        inT = [[cx.sb(st, f"inT{i}_{q}", [128, 16, 512], BF16) for q in range(nkq)] for i in range(nslot)]
        yT = cx.sb(st, "yT", [128, 16, 512], F32)
        xt = cx.sb(st, "xres", [128, 16, 512], F32)
        sq = cx.sb(st, "ysq", [128, 16, 512], BF16)
        rstd = cx.sb(st, "yrstd", [128, 512], F32)
        wsl = [cx.sb(st, f"w2sl{i}", [128, 16, 256], BF16) for i in range(3)]
        ones = consts["ones"]
        inv = inT_d.rearrange("(k p) s -> p k s", p=128)
        xsv = x_src.rearrange("(k p) s -> p k s", p=128)
        xdv = x_dst.rearrange("(k p) s -> p k s", p=128)
        wi = 0
        NT = SL // 512

        def load_in(tt):
            for q in range(nkq):
                b = inT[tt % nslot][q]
                for k0 in range(0, 16, 8):
                    sch.dma("sp", b.ap[:, k0:k0 + 8, :], inv[:, q * 16 + k0:q * 16 + k0 + 8, tt * 512:(tt + 1) * 512], b,
                            reads=[in_buf], writes=[b])

        def load_x(tt):
            for k0 in range(0, 16, 8):
                sch.dma("sp", xt.ap[:, k0:k0 + 8, :], xsv[:, k0:k0 + 8, tt * 512:(tt + 1) * 512], xt,
                        reads=[x_src_buf], writes=[xt])

        def epilogue(tt):
            rms_scale_tile(cx, st, yT, sq, rstd, ones, 16, 512, D)
            for k in range(16):
                sch.op("dve", lambda e, k=k: e.scalar_tensor_tensor(
                    yT.ap[:, k, :], yT.ap[:, k, :], g_col.ap[:, k:k + 1], rstd.ap[:, :], ALU.mult, ALU.mult),
                    reads=[yT, rstd, g_col], writes=[yT], selfdep=(k == 0))
            for k in range(16):
                sch.op("dve", lambda e, k=k: e.tensor_tensor(xt.ap[:, k, :], xt.ap[:, k, :], yT.ap[:, k, :], ALU.add),
                       reads=[yT, xt], writes=[xt])
            for k0 in range(0, 16, 8):
                sch.dma("sp", xdv[:, k0:k0 + 8, tt * 512:(tt + 1) * 512], xt.ap[:, k0:k0 + 8, :], xt,
                        reads=[xt], writes=[x_dst_buf])
            if tt + 1 < NT:
                load_x(tt + 1)

        load_in(0)
        load_x(0)
        for tt in range(NT):
            i_ = inT[tt % nslot]
            for cg in range(8):
                banks = [cx.bank(), cx.bank()]
                for kq in range(nkq):
                    w_ = wsl[wi % 3]
                    wi += 1
                    slab = cg * nkq + kq
                    if tt == 0:
                        wv = w_view_fn(kq * 2048, (kq + 1) * 2048, cg * 256, (cg + 1) * 256).rearrange("(k p) c -> p k c", p=128)
                        for k0 in range(0, 16, 8):
                            sch.dma("pool", w_.ap[:, k0:k0 + 8, :], wv[:, k0:k0 + 8, :], w_, writes=[w_])
                        sch.dma("sp", wc_d[slab], w_.ap[:, :, :], w_, reads=[w_], writes=[wc_buf])
                    else:
                        sch.dma("pool", w_.ap[:, :, :], wc_d[slab], w_, reads=[wc_buf], writes=[w_])
                    ib = i_[kq]
                    def mm(e, kq=kq, w_=w_, banks=banks, ib=ib):
                        ins = None
                        for cs in range(2):
                            for k in range(16):
                                ins = e.matmul(banks[cs].ap[:, :], w_.ap[:, k, cs * 128:(cs + 1) * 128],
                                               ib.ap[:, k, :], start=(kq == 0 and k == 0),
                                               stop=(kq == nkq - 1 and k == 15))
                        return ins
                    sch.op("pe", mm, reads=[ib, w_], writes=banks)
                if cg == 0 and tt > 0:
                    epilogue(tt - 1)
                for cs in range(2):
                    kk = cg * 2 + cs
                    if cs == 0:
                        sch.op("act", lambda e, b=banks[cs], kk=kk: e.activation(yT.ap[:, kk, :], b.ap[:, :], AF.Copy),
                               reads=[banks[cs]], writes=[yT])
                    else:
                        sch.op("dve", lambda e, b=banks[cs], kk=kk: e.tensor_copy(yT.ap[:, kk, :], b.ap[:, :]),
                               reads=[banks[cs]], writes=[yT])
            if tt + 1 < NT:
                load_in(tt + 1)
        epilogue(NT - 1)
        cx.end_phase()


def phase_conv(cx, pda, pdg, b_da, b_dg, hgat, b_hg, flag, cw, cb, cn, l, mixT, mix_buf, consts):
    sch = cx.sch
    with contextlib.ExitStack() as st:
        hb = [cx.sb(st, f"cv_h{i}", [128, 30 + SL], F32) for i in range(2)]
        gt = [cx.sb(st, f"cv_g{i}", [128, SL], F32) for i in range(2)]
        hg = [cx.sb(st, f"cv_hg{i}", [128, 32], F32) for i in range(2)]
        acc = [cx.sb(st, f"cv_acc{i}", [128, SL], F32) for i in range(4)]
        sq = cx.sb(st, "cv_sq", [128, 4, SL], BF16)
        rstd = cx.sb(st, "cv_rstd", [128, SL], F32)
        tmp = [cx.sb(st, f"cv_tmp{i}", [128, SL], F32) for i in range(2)]
        ob = [cx.sb(st, f"cv_ob{i}", [128, SL], BF16) for i in range(2)]
        ones = consts["ones"]
        for cc in range(4):
            h_, g_, hg_ = hb[cc % 2], gt[cc % 2], hg[cc % 2]
            eng = "dve"
            sch.dma("sp", h_.ap[:, 30:], pda[cc * 128:(cc + 1) * 128, :], h_, reads=[b_da], writes=[h_])
            sch.dma("sp", h_.ap[:, 0:30], hgat[cc * 128:(cc + 1) * 128, 2:32], h_, reads=[b_hg], writes=[h_])
            sch.dma("sp", hg_.ap[:, 0:30], hgat[GW + cc * 128:GW + (cc + 1) * 128, 2:32], hg_, reads=[b_hg], writes=[hg_])
            sch.dma("sp", g_.ap[:, :], pdg[cc * 128:(cc + 1) * 128, :], g_, reads=[b_dg], writes=[g_])
            sch.op("act", lambda e, g_=g_: e.activation(g_.ap[:, :], g_.ap[:, :], AF.Sigmoid), reads=[g_], writes=[g_])
            sch.op("act", lambda e, hg_=hg_: e.activation(hg_.ap[:, 0:30], hg_.ap[:, 0:30], AF.Sigmoid), reads=[hg_], writes=[hg_])
            sch.op(eng, lambda e, h_=h_, g_=g_: e.tensor_tensor(h_.ap[:, 30:], h_.ap[:, 30:], g_.ap[:, :], ALU.mult),
                   reads=[h_, g_], writes=[h_])
            sch.op(eng, lambda e, h_=h_, hg_=hg_: e.scalar_tensor_tensor(
                h_.ap[:, 0:30], h_.ap[:, 0:30], flag.ap[:, 0:1], hg_.ap[:, 0:30], ALU.mult, ALU.mult),
                reads=[h_, hg_, flag], writes=[h_])
            a_ = acc[cc]
            wbase = (l * 4 + cc) * CONVW
            bcol = cb.ap[:, l * 4 + cc:l * 4 + cc + 1]
            def conv(e, h_=h_, a_=a_, wbase=wbase, bcol=bcol):
                ins = e.tensor_scalar(a_.ap[:, :], h_.ap[:, 0:SL], cw.ap[:, wbase:wbase + 1], bcol, ALU.mult, ALU.add)
                for j in range(1, CONVW):
                    ins = e.scalar_tensor_tensor(a_.ap[:, :], h_.ap[:, j:j + SL], cw.ap[:, wbase + j:wbase + j + 1],
                                                 a_.ap[:, :], ALU.mult, ALU.add)
                return ins
            sch.op(eng, conv, reads=[h_, cw, cb], writes=[a_], selfdep=True)
            sch.op("act", lambda e, a_=a_, cc=cc: e.activation(sq.ap[:, cc, :], a_.ap[:, :], AF.Square),
                   reads=[a_], writes=[sq])
        for tt in range(SL // 512):
            bank = cx.bank()
            def mm(e, tt=tt, bank=bank):
                ins = None
                for cc in range(4):
                    ins = e.matmul(bank.ap[:, :], ones.ap[:, :], sq.ap[:, cc, tt * 512:(tt + 1) * 512],
                                   start=(cc == 0), stop=(cc == 3))
                return ins
            sch.op("pe", mm, reads=[sq, ones], writes=[bank])
            sch.op("dve", lambda e, tt=tt, bank=bank: e.tensor_scalar(
                rstd.ap[:, tt * 512:(tt + 1) * 512], bank.ap[:, :], 1.0 / GW, EPS, ALU.mult, ALU.add),
                reads=[bank], writes=[rstd])
        sch.op("act", lambda e: e.activation(rstd.ap[:, :], rstd.ap[:, :], AF.Sqrt), reads=[rstd], writes=[rstd])
        sch.op("dve", lambda e: e.reciprocal(rstd.ap[:, :], rstd.ap[:, :]), reads=[rstd], writes=[rstd])
        for cc in range(4):
            t_, o_ = tmp[cc % 2], ob[cc % 2]
            sch.op("dve", lambda e, cc=cc, t_=t_: e.scalar_tensor_tensor(
                t_.ap[:, :], acc[cc].ap[:, :], cn.ap[:, l * 4 + cc:l * 4 + cc + 1], rstd.ap[:, :], ALU.mult, ALU.mult),
                reads=[acc[cc], rstd, cn], writes=[t_], selfdep=True)
            sch.op("act", lambda e, t_=t_, o_=o_: e.activation(o_.ap[:, :], t_.ap[:, :], AF.Silu), reads=[t_], writes=[o_])
            sch.dma("sp", mixT[1536 + cc * 128:1536 + (cc + 1) * 128, :], o_.ap[:, :], o_, reads=[o_], writes=[mix_buf])
        cx.end_phase()


def phase_gmlp(cx, pu, pv, b_u, b_v, gw_d, gbT, l, mixT, mix_buf, consts):
    sch = cx.sch
    with contextlib.ExitStack() as st:
        wraw = cx.sb(st, "gm_wraw", [128, 8, 128], F32)
        WT = cx.sb(st, "gm_WT", [128, 8, 128], BF16)
        bfull = cx.sb(st, "gm_bfull", [128, GW], F32)
        ut = [cx.sb(st, f"gm_u{i}", [128, GW], F32) for i in range(2)]
        vt = [cx.sb(st, f"gm_v{i}", [128, GW], F32) for i in range(2)]
        vn = [cx.sb(st, f"gm_vn{i}", [128, GW], BF16) for i in range(2)]
        t32 = [cx.sb(st, f"gm_t{i}", [128, GW], F32) for i in range(2)]
        oa = [cx.sb(st, f"gm_oa{i}", [128, GW], BF16) for i in range(2)]
        oT = [cx.sb(st, f"gm_oT{i}", [128, 4, 128], BF16) for i in range(2)]
        st1 = [cx.sb(st, f"gm_s{i}", [128, 4], F32) for i in range(2)]
        identf, identb, tri = consts["identf"], consts["identb"], consts["tri"]
        pbf = consts["psum_bf"]
        sch.dma("sp", wraw.ap[:, :, :], gw_d[l], wraw, writes=[wraw])
        for g in range(8):
            sch.op("dve", lambda e, g=g: e.tensor_tensor(WT.ap[:, g, :], wraw.ap[:, g, :], tri.ap[:, :], ALU.mult),
                   reads=[wraw, tri], writes=[WT])
        sch.op("dve", lambda e: e.memset(bfull.ap[:, :], 0.0), writes=[bfull])
        for g in range(8):
            sch.op("dve", lambda e, g=g: e.tensor_scalar(
                bfull.ap[:, g * 64:(g + 1) * 64], bfull.ap[:, g * 64:(g + 1) * 64], gbT.ap[:, l * 8 + g:l * 8 + g + 1], 0.0, ALU.add, ALU.add),
                reads=[bfull, gbT], writes=[bfull], selfdep=(g == 0))
        for c in range(NBL):
            u_, v_, vn_, t_, oa_, oT_, s_ = ut[c % 2], vt[c % 2], vn[c % 2], t32[c % 2], oa[c % 2], oT[c % 2], st1[c % 2]
            sch.dma("sp", u_.ap[:, :], pu[c * 128:(c + 1) * 128, :], u_, reads=[b_u], writes=[u_])
            sch.dma("sp", v_.ap[:, :], pv[c * 128:(c + 1) * 128, :], v_, reads=[b_v], writes=[v_])
            sch.op("act", lambda e, u_=u_: e.activation(u_.ap[:, :], u_.ap[:, :], AF.Gelu_apprx_tanh), reads=[u_], writes=[u_])
            sch.op("act", lambda e, v_=v_: e.activation(v_.ap[:, :], v_.ap[:, :], AF.Gelu_apprx_tanh), reads=[v_], writes=[v_])
            def rs0(e, v_=v_, s_=s_):
                e.memset(s_.ap[:, 1:2], 0.0)
                return e.reduce_sum(s_.ap[:, 0:1], v_.ap[:, :], AX.X)
            sch.op("dve", rs0, reads=[v_], writes=[s_])
            sch.op("act", lambda e, s_=s_: e.activation(s_.ap[:, 0:1], s_.ap[:, 0:1], AF.Copy, scale=-1.0 / GW), reads=[s_], writes=[s_])
            sch.op("dve", lambda e, v_=v_, s_=s_: e.tensor_scalar(v_.ap[:, :], v_.ap[:, :], s_.ap[:, 0:1], 0.0, ALU.add, ALU.add),
                   reads=[v_, s_], writes=[v_])
            sch.op("act", lambda e, v_=v_, s_=s_, t_=t_: e.activation(t_.ap[:, :], v_.ap[:, :], AF.Square, accum_out=s_.ap[:, 1:2]),
                   reads=[v_], writes=[t_, s_])
            sch.op("dve", lambda e, s_=s_: e.tensor_scalar(s_.ap[:, 1:2], s_.ap[:, 1:2], 1.0 / GW, EPS, ALU.mult, ALU.add), reads=[s_], writes=[s_])
            sch.op("act", lambda e, s_=s_: e.activation(s_.ap[:, 1:2], s_.ap[:, 1:2], AF.Sqrt), reads=[s_], writes=[s_])
            sch.op("dve", lambda e, s_=s_: e.reciprocal(s_.ap[:, 2:3], s_.ap[:, 1:2]), reads=[s_], writes=[s_])
            sch.op("act", lambda e, v_=v_, s_=s_, vn_=vn_: e.activation(vn_.ap[:, :], v_.ap[:, :], AF.Copy, scale=s_.ap[:, 2:3]),
                   reads=[v_, s_], writes=[vn_])
            bank = cx.bank()
            def mm(e, vn_=vn_, bank=bank):
                ins = None
                for g in range(8):
                    ins = e.matmul(bank.ap[:, g * 64:(g + 1) * 64], WT.ap[:, g, :], vn_.ap[:, g * 64:(g + 1) * 64],
                                   start=True, stop=True)
                return ins
            sch.op("pe", mm, reads=[WT, vn_], writes=[bank])
            seq(sch, "dve", [
                (lambda e, bank=bank, t_=t_: e.tensor_tensor(t_.ap[:, :], bank.ap[:, :], bfull.ap[:, :], ALU.add), [bank, bfull], [t_]),
                (lambda e, t_=t_, u_=u_, oa_=oa_: e.tensor_tensor(oa_.ap[:, :], t_.ap[:, :], u_.ap[:, :], ALU.mult), [t_, u_], [oa_]),
            ])
            if c == 0 and cx.dbg is not None:
                for nm, b_ in [("u", u_), ("vn", vn_), ("t", t_), ("oa", oa_), ("bfull", bfull), ("v", v_)]:
                    sch.dma("sp", cx.dbg[nm], b_.ap[:, :], b_, reads=[b_], writes=[cx.dbgb])
            tb = cx.bank()
            def tr(e, oa_=oa_, tb=tb):
                ins = None
                for j in range(4):
                    ins = e.matmul(tb.ap[:, j * 128:(j + 1) * 128], oa_.ap[:, j * 128:(j + 1) * 128], identb.ap[:, :],
                                   start=True, stop=True)
                return ins
            sch.op("pe", tr, reads=[oa_, identb], writes=[tb])
            sch.op("act", lambda e, oT_=oT_, tb=tb: e.activation(oT_.ap[:, :, :], tb.ap[:, :].rearrange("p (j t) -> p j t", j=4), AF.Copy),
                   reads=[tb], writes=[oT_])
            sch.dma("sp", mixT[0:GW, c * 128:(c + 1) * 128].rearrange("(j p) t -> p j t", p=128), oT_.ap[:, :, :], oT_,
                    reads=[oT_], writes=[mix_buf])
        cx.end_phase()


def phase_attn(cx, kind, qT_d, kT_d, kprev_d, v_d, vprev_d, b_q, b_k, b_kg, b_v, b_vg, flag, l, mixT, mix_buf, mix_row0,
               consts, lam_bufs=None):
    sch = cx.sch
    diff = kind == "diff"
    E = 128 if diff else 64
    EA = E + 1
    alibi, ctab, tri, identb, pbf = consts["alibi"], consts["ctab"], consts["tri"], consts["identb"], consts["psum_bf"]
    DIFF_IDX = [0, 3, 6, 9]
    DIL_IDX = [1, 2, 4, 5, 7, 8, 10, 11]
    banks = cx.psum
    with contextlib.ExitStack() as st:
        KT = [cx.sb(st, f"at_K{i}", [128, S], BF16) for i in range(2)]
        QT = [cx.sb(st, f"at_Q{i}", [128, SL], BF16) for i in range(2)]
        nV = 1 if diff else 2
        VA = [[cx.sb(st, f"at_V{i}_{m}", [128, NB, EA], BF16) for m in range(nV)] for i in range(2)]
        PT = [cx.sb(st, f"at_P{i}", [128, 512], BF16) for i in range(4)]
        fin32 = [cx.sb(st, f"at_f{i}", [128, 2, 128], F32) for i in range(2)]
        fs = [cx.sb(st, f"at_s{i}", [128, 8], F32) for i in range(2)]
        ob = [cx.sb(st, f"at_ob{i}", [128, 128], BF16) for i in range(2)]
        oT = [cx.sb(st, f"at_oT{i}", [128, 512], BF16) for i in range(2)]
        for i in range(2):
            for m in range(nV):
                sch.op("dve", lambda e, i=i, m=m: e.memset(VA[i][m].ap[:, :, E:EA], 1.0), writes=[VA[i][m]])
        def acc_region(m, j):
            if diff:
                return m * 2 + j // 2, (j % 2) * EA
            return m, j * EA
        nacc = 4 if diff else 2
        accw = (2 if diff else 4) * EA
        asb = [cx.sb(st, f"at_acc{i}", [128, nacc, accw], F32) for i in range(2)]
        stb = [banks[4], banks[5], pbf]
        trb = banks[6]
        sti = 0
        pti = 0
        fi = 0
        def load_u(u):
            K_, Q_, V_ = KT[u % 2], QT[u % 2], VA[u % 2]
            sch.dma("sp", K_.ap[:, 0:SL], kprev_d[u * 128:(u + 1) * 128, :], K_, reads=[b_kg], writes=[K_])
            sch.dma("sp", K_.ap[:, SL:S], kT_d[u * 128:(u + 1) * 128, :], K_, reads=[b_k], writes=[K_])
            sch.dma("sp", Q_.ap[:, :], qT_d[u * 128:(u + 1) * 128, :], Q_, reads=[b_q], writes=[Q_])
            for m in range(nV):
                c0 = u * 128 + (0 if diff else m * 64)
                sch.dma("sp", V_[m].ap[:, 0:NBL, 0:E], vprev_d[:, c0:c0 + E].rearrange("(n p) e -> p n e", p=128), V_[m],
                        reads=[b_vg], writes=[V_[m]])
                sch.dma("sp", V_[m].ap[:, NBL:NB, 0:E], v_d[:, c0:c0 + E].rearrange("(n p) e -> p n e", p=128), V_[m],
                        reads=[b_v], writes=[V_[m]])
        load_u(0)
        for u in range(4):
            K_, Q_, V_ = KT[u % 2], QT[u % 2], VA[u % 2]
            for m in range(nV):
                sch.op("dve", lambda e, Vm=V_[m]: e.tensor_scalar(
                    Vm.ap[:, 0:NBL, :], Vm.ap[:, 0:NBL, :], flag.ap[:, 0:1], 0.0, ALU.mult, ALU.add),
                    reads=[V_[m], flag], writes=[V_[m]])
            if u + 1 < 4:
                load_u(u + 1)
            for qg in (2, 3):
                steps = []
                for m in range(2):
                    hidx = DIFF_IDX[u] if diff else DIL_IDX[u * 2 + m]
                    slope = 2.0 ** (-8.0 * (hidx + 1) / N_ALIBI)
                    fine = slope > 0.26
                    for kb in range(qg * 4 + 4):
                        steps.append((m, hidx, fine, kb))
                pend = []
                started = set()
                def emit_pv(stp, P_):
                    m, hidx, fine, kb = stp
                    jlo = max(0, kb - qg * 4)
                    Vm = V_[0] if diff else V_[m]
                    for j in range(jlo, 4):
                        bi, c0 = acc_region(m, j)
                        first = bi not in started
                        started.add(bi)
                        ab = banks[bi]
                        sch.op("pe", lambda e, ab=ab, c0=c0, P_=P_, Vm=Vm, j=j, kb=kb, first=first, qg=qg: e.matmul(
                            ab.ap[:, c0:c0 + EA], P_.ap[:, j * 128:(j + 1) * 128], Vm.ap[:, kb, :], start=first,
                            stop=(kb == qg * 4 + j), skip_group_check=True),
                            reads=[P_, Vm], writes=[ab])
                for stp in steps:
                    m, hidx, fine, kb = stp
                    jlo = max(0, kb - qg * 4)
                    c_lo = jlo * 128
                    sb_ = stb[sti % 3]
                    sti += 1
                    P_ = PT[pti % 4]
                    pti += 1
                    sch.op("pe", lambda e, sb_=sb_, K_=K_, Q_=Q_, m=m, kb=kb, c_lo=c_lo, qg=qg: e.matmul(
                        sb_.ap[:, c_lo:512], K_.ap[m * 64:(m + 1) * 64, kb * 128:(kb + 1) * 128],
                        Q_.ap[m * 64:(m + 1) * 64, (qg - 2) * 512 + c_lo:(qg - 1) * 512], start=True, stop=True),
                        reads=[K_, Q_], writes=[sb_])
                    if fine:
                        def ex(e, sb_=sb_, P_=P_, hidx=hidx, kb=kb, jlo=jlo, qg=qg):
                            ins = None
                            for j in range(jlo, 4):
                                v = 16 + (kb - (qg * 4 + j) + 15)
                                ins = e.activation(P_.ap[:, j * 128:(j + 1) * 128], sb_.ap[:, j * 128:(j + 1) * 128], AF.Exp,
                                                   bias=alibi.ap[:, hidx * 32 + v:hidx * 32 + v + 1])
                            return ins
                    else:
                        def ex(e, sb_=sb_, P_=P_, hidx=hidx, kb=kb, c_lo=c_lo, qg=qg):
                            v = kb - 4 * qg - 2 + 14
                            return e.activation(P_.ap[:, c_lo:512], sb_.ap[:, c_lo:512], AF.Exp,
                                                bias=alibi.ap[:, hidx * 32 + v:hidx * 32 + v + 1])
                    sch.op("act", ex, reads=[sb_, alibi], writes=[P_])
                    if diff:
                        if kb >= qg * 4:
                            sch.op("dve", lambda e, P_=P_, jlo=jlo: e.tensor_tensor(
                                P_.ap[:, jlo * 128:(jlo + 1) * 128], P_.ap[:, jlo * 128:(jlo + 1) * 128], tri.ap[:, :], ALU.mult),
                                reads=[P_, tri], writes=[P_])
                    else:
                        u0 = qg * 512 + c_lo - kb * 128
                        sch.op("dve", lambda e, P_=P_, c_lo=c_lo, u0=u0: e.tensor_tensor(
                            P_.ap[:, c_lo:512], P_.ap[:, c_lo:512], ctab.ap[:, u0:u0 + 512 - c_lo], ALU.mult),
                            reads=[P_, ctab], writes=[P_])
                    pend.append((stp, P_))
                    if len(pend) > 3:
                        emit_pv(*pend.pop(0))
                while pend:
                    emit_pv(*pend.pop(0))
                asb_ = asb[(u * 4 + qg) % 2]
                for bi in range(nacc):
                    sch.op("dve", lambda e, bi=bi, asb_=asb_: e.tensor_copy(asb_.ap[:, bi, :], banks[bi].ap[:, 0:accw]),
                           reads=[banks[bi]], writes=[asb_])
                oT_ = oT[(u * 4 + qg) % 2]
                for j in range(4):
                    f_, s_, ob_ = fin32[fi % 2], fs[fi % 2], ob[fi % 2]
                    fi += 1
                    b0, c00 = acc_region(0, j)
                    b1, c01 = acc_region(1, j)
                    a0 = Buf("a0v", asb_.ap[:, b0, c00:c00 + EA])
                    a1 = Buf("a1v", asb_.ap[:, b1, c01:c01 + EA])
                    if diff:
                        neg_lam, gsub = lam_bufs
                        def recs(e, a0=a0, a1=a1, s_=s_):
                            e.memset(s_.ap[:, 2:3], 0.0)
                            e.reciprocal(s_.ap[:, 0:1], a0.ap[:, E:EA])
                            return e.reciprocal(s_.ap[:, 1:2], a1.ap[:, E:EA])
                        sch.op("dve", recs, reads=[asb_], writes=[s_], selfdep=True)
                        def nrm(e, a0=a0, a1=a1, s_=s_, f_=f_):
                            e.activation(f_.ap[:, 0, :], a0.ap[:, 0:E], AF.Copy, scale=s_.ap[:, 0:1])
                            return e.activation(f_.ap[:, 1, :], a1.ap[:, 0:E], AF.Copy, scale=s_.ap[:, 1:2])
                        sch.op("act", nrm, reads=[asb_, s_], writes=[f_])
                        sch.op("dve", lambda e, f_=f_: e.scalar_tensor_tensor(f_.ap[:, 0, :], f_.ap[:, 1, :], neg_lam.ap[:, 0:1], f_.ap[:, 0, :],
                                                                              ALU.mult, ALU.add), reads=[f_, neg_lam], writes=[f_])
                        sch.op("act", lambda e, f_=f_, s_=s_: e.activation(f_.ap[:, 1, :], f_.ap[:, 0, :], AF.Square, accum_out=s_.ap[:, 2:3]),
                               reads=[f_], writes=[f_, s_])
                        sch.op("dve", lambda e, s_=s_: e.tensor_scalar(s_.ap[:, 2:3], s_.ap[:, 2:3], 1.0 / 128, EPS, ALU.mult, ALU.add), reads=[s_], writes=[s_])
                        sch.op("act", lambda e, s_=s_: e.activation(s_.ap[:, 2:3], s_.ap[:, 2:3], AF.Sqrt), reads=[s_], writes=[s_])
                        sch.op("dve", lambda e, s_=s_: e.reciprocal(s_.ap[:, 3:4], s_.ap[:, 2:3]), reads=[s_], writes=[s_])
                        sch.op("act", lambda e, f_=f_, s_=s_: e.activation(f_.ap[:, 1, :], f_.ap[:, 0, :], AF.Copy, scale=s_.ap[:, 3:4]),
                               reads=[f_, s_], writes=[f_])
                        sch.op("dve", lambda e, f_=f_, ob_=ob_: e.tensor_tensor(ob_.ap[:, :], f_.ap[:, 1, :], gsub.ap[:, :], ALU.mult),
                               reads=[f_, gsub], writes=[ob_])
                    else:
                        def recs(e, a0=a0, a1=a1, s_=s_):
                            e.memset(s_.ap[:, 2:3], 0.0)
                            e.reciprocal(s_.ap[:, 0:1], a0.ap[:, E:EA])
                            return e.reciprocal(s_.ap[:, 1:2], a1.ap[:, E:EA])
                        sch.op("dve", recs, reads=[asb_], writes=[s_], selfdep=True)
                        def nrm(e, a0=a0, a1=a1, s_=s_, ob_=ob_):
                            e.activation(ob_.ap[:, 0:64], a0.ap[:, 0:E], AF.Copy, scale=s_.ap[:, 0:1])
                            return e.activation(ob_.ap[:, 64:128], a1.ap[:, 0:E], AF.Copy, scale=s_.ap[:, 1:2])
                        sch.op("act", nrm, reads=[asb_, s_], writes=[ob_])
                    sch.op("pe", lambda e, ob_=ob_, j=j: e.matmul(trb.ap[:, j * 128:(j + 1) * 128], ob_.ap[:, :], identb.ap[:, :],
                                                                  start=True, stop=True),
                           reads=[ob_, identb], writes=[trb])
                sch.op("act", lambda e, oT_=oT_: e.activation(oT_.ap[:, :], trb.ap[:, :], AF.Copy), reads=[trb], writes=[oT_])
                sch.dma("sp", mixT[mix_row0 + u * 128:mix_row0 + (u + 1) * 128, (qg - 2) * 512:(qg - 1) * 512], oT_.ap[:, :], oT_,
                        reads=[oT_], writes=[mix_buf])
        cx.end_phase()


def build(nlayers, first_layer=0, debug=False):
    nc = bass.Bass("TRN2", target_bir_lowering=False)
    skind = "ExternalOutput" if debug else "Internal"
    L = DEPTH
    ein = lambda n, shp, dt=F32: nc.dram_tensor(n, list(shp), dt, kind="ExternalInput").ap()
    scr = lambda n, shp, dt: nc.dram_tensor(n, list(shp), dt, kind=skind).ap()
    xT_in = ein("xT", [D, SL])
    flag_in = ein("flag", [128, 1])
    w_in = ein("w_in", [L, D, INW])
    w_out = ein("w_out", [L, D, D])
    w_ff1 = ein("w_ff1", [L, D, DFF])
    w_ff2 = ein("w_ff2", [L, DFF, D])
    gmlp_w = ein("gmlp_w", [L, 128, 8, 128])
    gcols = ein("gcols", [128, 4 * L * 16])
    c_f32 = ein("c_f32", [128, 384 + 128])
    c_bf = ein("c_bf", [128, 128 + 128 + 128 + S])
    p_cols = ein("p_cols", [128, L * 4 * CONVW + L * 4 + L * 4 + L * 8])
    p_rep = ein("p_rep", [128, L * 256 + L * 128])
    outT = nc.dram_tensor("outT", [D, SL], F32, kind="ExternalOutput").ap()
    pu, pv = scr("s_u", [SL, GW], F32), scr("s_v", [SL, GW], F32)
    pbq, pcq = scr("s_bq", [GW, SL], BF16), scr("s_cq", [GW, SL], BF16)
    ks_t = nc.dram_tensor("s_ksend", [2 * GW, SL], BF16, kind="Internal")
    kg_t = nc.dram_tensor("s_kgat", [4 * GW, SL], BF16, kind="Internal")
    vs_t = nc.dram_tensor("s_vsend", [2 * SL, GW], BF16, kind="Internal")
    vg_t = nc.dram_tensor("s_vgat", [4 * SL, GW], BF16, kind="Internal")
    hs_t = nc.dram_tensor("s_hsend", [2 * GW, 32], F32, kind="Internal")
    hg_t = nc.dram_tensor("s_hgat", [4 * GW, 32], F32, kind="Internal")
    ksend, kgat, vsend, vgat, hsend, hgat = (t.ap() for t in (ks_t, kg_t, vs_t, vg_t, hs_t, hg_t))
    pbk, pck = ksend[0:GW, :], ksend[GW:2 * GW, :]
    pbv, pcv = vsend[0:SL, :], vsend[SL:2 * SL, :]
    pda, pdg = scr("s_da", [GW, SL], F32), scr("s_dg", [GW, SL], F32)
    mixT = scr("s_mixT", [D, SL], BF16)
    fT = scr("s_fT", [DFF, SL], BF16)
    xA, xB = scr("s_xA", [D, SL], F32), scr("s_xB", [D, SL], F32)
    wcache = scr("s_wc", [32, 128, 16, 256], BF16)

    with contextlib.ExitStack() as stack:
        sch = Sched(nc, stack)
        cx = Ctx(nc, sch, stack)
        for i in range(7):
            t = stack.enter_context(nc.psum_tensor(f"ps{i}", [128, 512], F32))
            cx.psum.append(Buf(f"ps{i}", t))
        if debug:
            cx.dbg = {n: nc.dram_tensor("dbg_" + n, shp, dt, kind="ExternalOutput").ap() for n, shp, dt in [
                ("u", [128, 512], F32), ("vn", [128, 512], BF16), ("t", [128, 512], F32), ("oa", [128, 512], BF16),
                ("bfull", [128, 512], F32), ("v", [128, 512], F32), ("WT", [128, 8, 128], BF16)]}
            cx.dbgb = dbuf("dbgb")
        consts = {}
        pbf_t = stack.enter_context(nc.psum_tensor("psbf", [128, 512], F32))
        consts["psum_bf"] = Buf("psbf", pbf_t)
        cf = cx.sb(stack, "c_f32", [128, 512], F32)
        cb_ = cx.sb(stack, "c_bf", [128, 384 + S], BF16)
        pc = cx.sb(stack, "p_cols", [128, L * 4 * CONVW + L * 16], F32)
        pr = cx.sb(stack, "p_rep", [128, L * 384], F32)
        gc = cx.sb(stack, "gc", [128, 4 * L * 16], F32)
        flag = cx.sb(stack, "flag", [128, 1], F32)
        ccs = [cx.sb(stack, f"ccslot{i}", [128, 1], F32) for i in range(4)]
        sch.dma("sp", cf.ap[:, :], c_f32[:, :], cf, writes=[cf])
        sch.dma("pool", cb_.ap[:, :], c_bf[:, :], cb_, writes=[cb_])
        sch.dma("sp", pc.ap[:, :], p_cols[:, :], pc, writes=[pc])
        sch.dma("sp", pr.ap[:, :], p_rep[:, :], pr, writes=[pr])
        sch.dma("sp", gc.ap[:, :], gcols[:, :], gc, writes=[gc])
        sch.dma("sp", flag.ap[:, :], flag_in[:, :], flag, writes=[flag])
        def view(parent, ap):
            b = Buf(parent.name + "_v", ap)
            b.w = parent.w
            return b
        consts["alibi"] = view(cf, cf.ap[:, 0:384])
        consts["identf"] = view(cf, cf.ap[:, 384:512])
        consts["ones"] = view(cb_, cb_.ap[:, 0:128])
        consts["identb"] = view(cb_, cb_.ap[:, 128:256])
        consts["tri"] = view(cb_, cb_.ap[:, 256:384])
        consts["ctab"] = view(cb_, cb_.ap[:, 384:384 + S])
        o1 = L * 4 * CONVW
        cw = view(pc, pc.ap[:, 0:o1])
        cbias = view(pc, pc.ap[:, o1:o1 + L * 4])
        cnorm = view(pc, pc.ap[:, o1 + L * 4:o1 + L * 8])
        gbT = view(pc, pc.ap[:, o1 + L * 8:o1 + L * 16])
        lamw = cx.sb(stack, "lamw", [128, 16], F32)
        lamt = cx.sb(stack, "lamt", [128, 128], F32)
        lamt2 = cx.sb(stack, "lamt2", [128, 128], F32)
        gsub = cx.sb(stack, "gsub", [128, 128], F32)
        neg_lam = cx.sb(stack, "neg_lam", [128, 1], F32)
        cx.cur = []

        bx = {n: dbuf("b_" + n) for n in ["xin", "xA", "xB", "out", "u", "v", "bq", "bk", "bv", "cq", "ck", "cv", "da", "dg", "mix", "f", "wc", "kg", "vg", "hs", "hg"]}
        x_cur, x_cur_b = xT_in, bx["xin"]
        for li in range(nlayers):
            l = first_layer + li
            last = li == nlayers - 1
            g = lambda which: view(gc, gc.ap[:, (which * L + l) * 16:(which * L + l) * 16 + 16])
            outs = [
                dict(mode="tok", dst=pu, dt=F32, dbuf=bx["u"]),
                dict(mode="tok", dst=pv, dt=F32, dbuf=bx["v"]),
                dict(mode="feat", dst=pbq, dt=BF16, scale=0.125, dbuf=bx["bq"]),
                dict(mode="feat", dst=pbk, dt=BF16, dbuf=bx["bk"]),
                dict(mode="tok", dst=pbv, dt=BF16, dbuf=bx["bv"]),
                dict(mode="feat", dst=pcq, dt=BF16, scale=0.125, dbuf=bx["cq"]),
                dict(mode="feat", dst=pck, dt=BF16, dbuf=bx["ck"]),
                dict(mode="tok", dst=pcv, dt=BF16, dbuf=bx["cv"]),
                dict(mode="feat", dst=pda, dt=F32, dbuf=bx["da"]),
                dict(mode="feat", dst=pdg, dt=F32, dbuf=bx["dg"]),
            ]
            order = [3, 4, 6, 7, 8, 9, 0, 1, 2, 5]
            def gather_kv():
                sch.cc(ks_t.ap().opt(), kg_t.ap().opt(), ccs[0], reads=[bx["bk"], bx["ck"]], writes=[bx["kg"]])
                sch.cc(vs_t.ap().opt(), vg_t.ap().opt(), ccs[1], reads=[bx["bv"], bx["cv"]], writes=[bx["vg"]])
            def gather_halo():
                sch.dma("sp", hsend[0:GW, :], pda[:, SL - 32:SL], ccs[3], reads=[bx["da"]], writes=[bx["hs"]])
                sch.dma("sp", hsend[GW:2 * GW, :], pdg[:, SL - 32:SL], ccs[3], reads=[bx["dg"]], writes=[bx["hs"]])
                sch.cc(hs_t.ap().opt(), hg_t.ap().opt(), ccs[2], reads=[bx["hs"]], writes=[bx["hg"]])
            phase_norm_proj(cx, x_cur, x_cur_b, g(0), lambda i, l=l: w_in[l, :, order[i] * 512:(order[i] + 1) * 512], 10,
                            lambda i: outs[order[i]], consts, post_load={6: gather_kv, 8: gather_halo})
            lam_init = 0.8 - 0.6 * math.exp(-0.3 * l)
            lp = pr.ap[:, l * 256:(l + 1) * 256]
            def lam1(e, lp=lp):
                e.memset(lamw.ap[:, 0:2], 0.0)
                e.tensor_tensor(lamt.ap[:, 0:64], lp[:, 0:64], lp[:, 64:128], ALU.mult)
                return e.tensor_tensor(lamt.ap[:, 64:128], lp[:, 128:192], lp[:, 192:256], ALU.mult)
            sch.op("dve", lam1, reads=[pr], writes=[lamt, lamw])
            def lam2(e):
                e.activation(lamt2.ap[:, 0:64], lamt.ap[:, 0:64], AF.Copy, accum_out=lamw.ap[:, 0:1])
                return e.activation(lamt2.ap[:, 64:128], lamt.ap[:, 64:128], AF.Copy, accum_out=lamw.ap[:, 1:2])
            sch.op("act", lam2, reads=[lamt], writes=[lamt2, lamw])
            sch.op("dve", lambda e: e.tensor_copy(lamw.ap[:, 2:4], lamw.ap[:, 0:2]), reads=[lamw], writes=[lamw])
            sch.op("act", lambda e: e.activation(lamw.ap[:, 4:6], lamw.ap[:, 2:4], AF.Exp), reads=[lamw], writes=[lamw])
            sch.op("dve", lambda e, lam_init=lam_init: e.scalar_tensor_tensor(
                neg_lam.ap[:, 0:1], lamw.ap[:, 5:6], -lam_init, lamw.ap[:, 4:5], ALU.add, ALU.subtract), reads=[lamw], writes=[neg_lam])
            sch.op("dve", lambda e, l=l, lam_init=lam_init: e.tensor_scalar(
                gsub.ap[:, :], pr.ap[:, L * 256 + l * 128:L * 256 + (l + 1) * 128], 1.0 - lam_init, None, ALU.mult), reads=[pr], writes=[gsub])
            phase_gmlp(cx, pu, pv, bx["u"], bx["v"], gmlp_w, gbT, l, mixT, bx["mix"], consts)
            phase_attn(cx, "diff", pbq, pbk, kgat[0:GW, :], pbv, vgat[0:SL, :], bx["bq"], bx["bk"], bx["kg"], bx["bv"], bx["vg"],
                       flag, l, mixT, bx["mix"], 512, consts, lam_bufs=(neg_lam, gsub))
            phase_attn(cx, "dil", pcq, pck, kgat[GW:2 * GW, :], pcv, vgat[SL:2 * SL, :], bx["cq"], bx["ck"], bx["kg"], bx["cv"],
                       bx["vg"], flag, l, mixT, bx["mix"], 1024, consts)
            phase_conv(cx, pda, pdg, bx["da"], bx["dg"], hgat, bx["hg"], flag, cw, cbias, cnorm, l, mixT, bx["mix"], consts)
            phase_proj_norm_res(cx, mixT, bx["mix"], 16, lambda r0, r1, c0, c1, l=l: w_out[l, r0:r1, c0:c1], g(1),
                                x_cur, x_cur_b, xA, bx["xA"], consts, wcache, bx["wc"])
            fouts = [dict(mode="feat", dst=fT[i * 512:(i + 1) * 512, :], dt=BF16, relu2=True, dbuf=bx["f"]) for i in range(16)]
            phase_norm_proj(cx, xA, bx["xA"], g(2), lambda i, l=l: w_ff1[l, :, i * 512:(i + 1) * 512], 16,
                            lambda i: fouts[i], consts)
            xo, xob = (outT, bx["out"]) if last else (xB, bx["xB"])
            phase_proj_norm_res(cx, fT, bx["f"], 64, lambda r0, r1, c0, c1, l=l: w_ff2[l, r0:r1, c0:c1], g(3),
                                xA, bx["xA"], xo, xob, consts, wcache, bx["wc"])
            x_cur, x_cur_b = xB, bx["xB"]
        sch.barrier()
        sch.emit()
    return nc


def _count(d):
    return (d <= 128) * 1.0 + ((d % 4 == 0) & (d <= 512)) * 1.0 + ((d % 16 == 0) & (d <= 2048)) * 1.0


def host_consts():
    ki = np.arange(128, dtype=np.float64)[:, None]
    slopes = 2.0 ** (-8.0 * np.arange(1, N_ALIBI + 1, dtype=np.float64) / N_ALIBI)
    off = np.zeros(32)
    off[:16] = 128.0 * (np.arange(16) - 14)
    off[16:] = 128.0 * (np.arange(16) - 15) - 64.0
    alibi = (slopes[None, :, None] * (ki[:, :, None] + off[None, None, :])).reshape(128, 384)
    c_f32 = np.concatenate([alibi, np.eye(128)], axis=1).astype(np.float32)
    uu = np.arange(S)[None, :]
    dd = uu - np.arange(128)[:, None]
    ctab = np.where(dd >= 0, _count(np.maximum(dd, 0)), 0.0)
    tri = (np.arange(128)[None, :] >= np.arange(128)[:, None]) * 1.0
    c_bf = np.concatenate([np.ones((128, 128)), np.eye(128), tri, ctab], axis=1).astype(np.float32)
    return c_f32, c_bf


def host_params(p):
    L = DEPTH
    gc = np.stack([p["g_mix_pre"], p["g_mix_post"], p["g_ffn_pre"], p["g_ffn_post"]], 0)
    gcols = np.ascontiguousarray(gc.reshape(4, L, 16, 128).transpose(3, 0, 1, 2).reshape(128, 4 * L * 16))
    cw = p["conv_w"].reshape(L, CONVW, 4, 128).transpose(3, 0, 2, 1).reshape(128, L * 4 * CONVW)
    cb = p["conv_b"].reshape(L, 4, 128).transpose(2, 0, 1).reshape(128, L * 4)
    cn = p["conv_norm"].reshape(L, 4, 128).transpose(2, 0, 1).reshape(128, L * 4)
    gb = p["gmlp_b"].transpose(2, 0, 1).reshape(128, L * 8)
    p_cols = np.ascontiguousarray(np.concatenate([cw, cb, cn, gb], axis=1).astype(np.float32))
    lam = np.broadcast_to(p["diff_lam"].reshape(1, L * 256), (128, L * 256))
    sub = np.broadcast_to(p["diff_subln"].reshape(1, L * 128), (128, L * 128))
    p_rep = np.ascontiguousarray(np.concatenate([lam, sub], axis=1).astype(np.float32))
    return gcols, p_cols, p_rep


def make_in_maps(inputs, x_per_core):
    p = {k: np.asarray(v, dtype=np.float32) for k, v in inputs.items()}
    c_f32, c_bf = host_consts()
    gcols, p_cols, p_rep = host_params(p)
    maps = []
    for xc in x_per_core:
        maps.append({
            "flag": np.full((128, 1), float(len(maps) % 2), dtype=np.float32),
            "xT": np.ascontiguousarray(xc.T), "w_in": p["w_in"], "w_out": p["w_out"], "w_ff1": p["w_ff1"],
            "w_ff2": p["w_ff2"], "gmlp_w": np.ascontiguousarray(p["gmlp_w"].transpose(0, 3, 1, 2)), "gcols": gcols, "c_f32": c_f32, "c_bf": c_bf,
            "p_cols": p_cols, "p_rep": p_rep,
        })
    return maps


N_LAUNCH_LAYERS = 2


def kernel(**inputs):
    x = np.asarray(inputs["x"], dtype=np.float32)
    B = x.shape[0]
    cur = [x[c // 2, (c % 2) * SL:(c % 2 + 1) * SL] for c in range(8)]
    for l0 in range(0, DEPTH, N_LAUNCH_LAYERS):
        nc = build(N_LAUNCH_LAYERS, first_layer=l0)
        maps = make_in_maps(inputs, cur)
        res = run_bass_kernel_spmd(nc, maps, core_ids=list(range(8))).results
        cur = [np.ascontiguousarray(r["outT"].T) for r in res]
    out = np.stack([np.concatenate([cur[2 * b], cur[2 * b + 1]], axis=0) for b in range(B)], 0)
    return out.astype(np.float32)
```

```python
import contextlib
import math
import numpy as np
import concourse.bass as bass
import concourse.mybir as mybir
from concourse.bass_utils import run_bass_kernel_spmd

F32 = mybir.dt.float32
BF16 = mybir.dt.bfloat16
AF = mybir.ActivationFunctionType
ALU = mybir.AluOpType
AX = mybir.AxisListType

D = 2048
S = 2048
NB = S // 128
SL = S // 2
NBL = SL // 128
PAIRS = [[0, 1], [2, 3], [4, 5], [6, 7]]
DEPTH = 2
GW = 512
INW = 10 * GW
DFF = 4 * D
EPS = 1e-6
N_ALIBI = 12
CONVW = 31


class Buf:
    def __init__(self, name, ap=None, sem=None):
        self.name = name
        self.ap = ap
        self.sem = sem
        self.dcount = 0
        self.w = {}
        self.r = {}
        self.multi = False

    def __getitem__(self, k):
        return self.ap[k]


def _merge(dst, evs):
    for k, (s, v) in evs.items():
        if k not in dst or dst[k][1] < v:
            dst[k] = (s, v)


class Sched:
    ENG = ("pe", "act", "dve", "pool", "sp")

    def __init__(self, nc, stack):
        self.nc = nc
        self.stack = stack
        self.prog = {e: [] for e in self.ENG}
        self.esem = {e: stack.enter_context(nc.semaphore("es_" + e)) for e in self.ENG}
        self.ecount = {e: 0 for e in self.ENG}
        self.waited = {e: {} for e in self.ENG}
        self.allev = {}
        self.nsem = 5
        self.trace = {e: [] for e in self.ENG}
        self.free_sems = []

    def newsem(self, name):
        if self.free_sems:
            return self.free_sems.pop()
        self.nsem += 1
        return self.stack.enter_context(self.nc.semaphore(name)), 0

    def release(self, bufs):
        for b in bufs:
            if b.sem is not None:
                self.free_sems.append((b.sem, b.dcount))
                b.sem = None

    def _deps(self, eng, reads, writes):
        deps = {}
        for b in reads:
            _merge(deps, b.w)
        for b in writes:
            if not b.multi:
                _merge(deps, b.w)
            _merge(deps, b.r)
        return deps

    def _emit_waits(self, eng, deps, selfdep=False):
        me = id(self.esem[eng])
        w = self.waited[eng]
        for k, (s, v) in deps.items():
            if k == me and not selfdep:
                continue
            if w.get(k, 0) >= v:
                continue
            w[k] = v
            self.trace[eng].append(("wait", k, v))
            self.prog[eng].append(lambda e, s=s, v=v: e.wait_ge(s, v))

    def _finish(self, ev, reads, writes):
        for b in writes:
            if b.multi:
                _merge(b.w, ev)
            else:
                b.w = dict(ev)
            b.r = {}
        for b in reads:
            _merge(b.r, ev)
        _merge(self.allev, ev)

    def op(self, eng, fn, reads=(), writes=(), selfdep=False):
        deps = self._deps(eng, reads, writes)
        self._emit_waits(eng, deps, selfdep)
        self.ecount[eng] += 1
        s, v = self.esem[eng], self.ecount[eng]
        self.trace[eng].append(("inc", id(s), 1))
        self.prog[eng].append(lambda e, fn=fn, s=s: fn(e).then_inc(s, 1))
        self._finish({id(s): (s, v)}, reads, writes)

    def dma(self, q, out_ap, in_ap, slot, reads=(), writes=()):
        deps = self._deps(q, reads, writes)
        self._emit_waits(q, deps)
        if slot.sem is None:
            slot.sem, slot.dcount = self.newsem("d_" + slot.name)
        slot.dcount += 16
        s, v = slot.sem, slot.dcount
        self.trace[q].append(("inc", id(s), 16))
        self.prog[q].append(lambda e, o=out_ap, i=in_ap, s=s: e.dma_start(out=o, in_=i).then_inc(s, 16))
        self._finish({id(s): (s, v)}, reads, writes)

    def cc(self, in_ap, out_ap, slot, reads=(), writes=()):
        deps = self._deps("pool", reads, writes)
        self._emit_waits("pool", deps)
        if slot.sem is None:
            slot.sem, slot.dcount = self.newsem("c_" + slot.name)
        slot.dcount += 1
        s, v = slot.sem, slot.dcount
        self.trace["pool"].append(("inc", id(s), 1))
        self.prog["pool"].append(lambda e, o=out_ap, i=in_ap, s=s: e.collective_compute(
            "AllGather", ALU.bypass, replica_groups=PAIRS, ins=[i], outs=[o]).then_inc(s, 1))
        self._finish({id(s): (s, v)}, reads, writes)

    def barrier(self, engines=None):
        for e in (engines or self.ENG):
            self._emit_waits(e, self.allev)

    def emit(self):
        nc = self.nc
        with nc.Block() as block:
            @block.tensor
            def _(e):
                for f in self.prog["pe"]:
                    f(e)

            @block.scalar
            def _(e):
                for f in self.prog["act"]:
                    f(e)

            @block.vector
            def _(e):
                for f in self.prog["dve"]:
                    f(e)

            @block.gpsimd
            def _(e):
                for f in self.prog["pool"]:
                    f(e)

            @block.sync
            def _(e):
                for f in self.prog["sp"]:
                    f(e)


class Ctx:
    def __init__(self, nc, sch, stack):
        self.nc, self.sch, self.stack = nc, sch, stack
        self.psum = []
        self.pi = 0
        self.wi = 0
        self.ei = 0
        self.dbg = None
        self.dbgb = None
        self.cur = []

    def sb(self, stack, name, shape, dt):
        self.wi += 1
        name = f"sb{self.wi}_{name}"
        t = stack.enter_context(self.nc.sbuf_tensor(name, list(shape), dt))
        b = Buf(name, t)
        self.cur.append(b)
        return b

    def end_phase(self):
        self.sch.barrier()
        self.sch.release(self.cur)
        self.cur = []

    def bank(self):
        b = self.psum[self.pi % len(self.psum)]
        self.pi += 1
        return b

    def evac_eng(self):
        self.ei += 1
        return "act" if self.ei % 2 else "dve"


def load_weight_slab(cx, wslot, w_dram_view, kchunks):
    v = w_dram_view.rearrange("(k p) c -> p k c", p=128)
    step = 4
    for k0 in range(0, kchunks, step):
        cx.sch.dma("pool", wslot.ap[:, k0:k0 + step, :], v[:, k0:k0 + step, :], wslot, writes=[wslot])


def rms_scale_tile(cx, st, xt, sq, rstd, ones, nk, ncols, dim):
    sch = cx.sch
    bank = cx.bank()
    for k in range(nk):
        sch.op("act", lambda e, k=k: e.activation(sq.ap[:, k, :ncols], xt.ap[:, k, :ncols], AF.Square),
               reads=[xt], writes=[sq])
    def mm(e):
        ins = None
        for k in range(nk):
            ins = e.matmul(bank.ap[:, :ncols], ones.ap[:, :], sq.ap[:, k, :ncols], start=(k == 0), stop=(k == nk - 1))
        return ins
    sch.op("pe", mm, reads=[sq, ones], writes=[bank])
    sch.op("dve", lambda e: e.tensor_scalar(rstd.ap[:, :ncols], bank.ap[:, :ncols], 1.0 / dim, EPS, ALU.mult, ALU.add),
           reads=[bank], writes=[rstd])
    sch.op("act", lambda e: e.activation(rstd.ap[:, :ncols], rstd.ap[:, :ncols], AF.Sqrt), reads=[rstd], writes=[rstd])
    sch.op("dve", lambda e: e.reciprocal(rstd.ap[:, :ncols], rstd.ap[:, :ncols]), reads=[rstd], writes=[rstd])


def phase_norm_proj(cx, xT_d, xT_buf, g_col, w_view_fn, nslabs, slab_out, consts, post_load=None):
    nc, sch = cx.nc, cx.sch
    with contextlib.ExitStack() as st:
        hT = cx.sb(st, "hT", [128, 16, SL], BF16)
        xt = [cx.sb(st, f"xt{i}", [128, 16, 256], F32) for i in range(2)]
        sq = [cx.sb(st, f"sq{i}", [128, 16, 256], BF16) for i in range(2)]
        rstd = [cx.sb(st, f"rstd{i}", [128, 256], F32) for i in range(2)]
        wsl = [cx.sb(st, f"wsl{i}", [128, 16, 512], BF16) for i in range(2)]
        stg32 = [cx.sb(st, f"stg32_{i}", [128, 512], F32) for i in range(3)]
        stg16 = [cx.sb(st, f"stg16_{i}", [128, 512], BF16) for i in range(3)]
        tmp32 = [cx.sb(st, f"tmp32_{i}", [128, 512], F32) for i in range(2)]
        ones = consts["ones"]
        TT = 256
        xv = xT_d.rearrange("(k p) s -> p k s", p=128)
        for t in range(SL // TT):
            x_, sq_, r_ = xt[t % 2], sq[t % 2], rstd[t % 2]
            for k0 in range(0, 16, 8):
                sch.dma("sp", x_.ap[:, k0:k0 + 8, :], xv[:, k0:k0 + 8, t * TT:(t + 1) * TT], x_,
                        reads=[xT_buf], writes=[x_])
            rms_scale_tile(cx, st, x_, sq_, r_, ones, 16, TT, D)
            for k in range(16):
                sch.op("dve", lambda e, k=k, x_=x_, r_=r_, t=t: e.scalar_tensor_tensor(
                    hT.ap[:, k, t * TT:(t + 1) * TT], x_.ap[:, k, :], g_col.ap[:, k:k + 1], r_.ap[:, :],
                    ALU.mult, ALU.mult), reads=[x_, r_, g_col], writes=[hT], selfdep=(k == 0))
        si = 0
        for i in range(nslabs):
            w_ = wsl[i % 2]
            load_weight_slab(cx, w_, w_view_fn(i), 16)
            if post_load and i in post_load:
                post_load[i]()
            so = slab_out(i)
            groups = []
            if so["mode"] == "tok":
                for tb in range(NBL):
                    groups.append(("tok", tb, None))
            else:
                for cs in range(4):
                    for tt in range(SL // 512):
                        groups.append(("feat", cs, tt))
            for (mode, a, b) in groups:
                bank = cx.bank()
                if mode == "tok":
                    def mm(e, a=a, w_=w_, bank=bank):
                        ins = None
                        for k in range(16):
                            ins = e.matmul(bank.ap[:, :], hT.ap[:, k, a * 128:(a + 1) * 128], w_.ap[:, k, :],
                                           start=(k == 0), stop=(k == 15))
                        return ins
                    dst = so["dst"][a * 128:(a + 1) * 128, :]
                else:
                    def mm(e, a=a, b=b, w_=w_, bank=bank):
                        ins = None
                        for k in range(16):
                            ins = e.matmul(bank.ap[:, :], w_.ap[:, k, a * 128:(a + 1) * 128],
                                           hT.ap[:, k, b * 512:(b + 1) * 512], start=(k == 0), stop=(k == 15))
                        return ins
                    dst = so["dst"][a * 128:(a + 1) * 128, b * 512:(b + 1) * 512]
                sch.op("pe", mm, reads=[hT, w_], writes=[bank])
                stg = (stg32 if so["dt"] == F32 else stg16)[si % 3]
                si += 1
                if so.get("relu2"):
                    t32 = tmp32[si % 2]
                    sch.op("act", lambda e, bank=bank, t32=t32: e.activation(t32.ap[:, :], bank.ap[:, :], AF.Relu),
                           reads=[bank], writes=[t32])
                    sch.op("dve", lambda e, t32=t32, stg=stg: e.tensor_tensor(stg.ap[:, :], t32.ap[:, :], t32.ap[:, :], ALU.mult),
                           reads=[t32], writes=[stg])
                else:
                    eng = cx.evac_eng()
                    sc = so.get("scale", 1.0)
                    if eng == "act":
                        sch.op("act", lambda e, bank=bank, stg=stg, sc=sc: e.activation(
                            stg.ap[:, :], bank.ap[:, :], AF.Copy, scale=sc), reads=[bank], writes=[stg])
                    else:
                        sch.op("dve", lambda e, bank=bank, stg=stg, sc=sc: e.tensor_scalar(
                            stg.ap[:, :], bank.ap[:, :], sc, None, ALU.mult), reads=[bank], writes=[stg])
                sch.dma("sp", dst, stg.ap[:, :], stg, reads=[stg], writes=[so["dbuf"]])
        cx.end_phase()


def seq(sch, eng, items):
    for n, (fn, r, w) in enumerate(items):
        sch.op(eng, fn, reads=r, writes=w, selfdep=(n > 0))


def dbuf(name):
    b = Buf(name)
    b.multi = True
    return b


def phase_proj_norm_res(cx, inT_d, in_buf, nk, w_view_fn, g_col, x_src, x_src_buf, x_dst, x_dst_buf, consts, wc_d, wc_buf):
    sch = cx.sch
    nkq = nk // 16
    nslot = 2 if nk == 16 else 1
    with contextlib.ExitStack() as st:
        inT = [[cx.sb(st, f"inT{i}_{q}", [128, 16, 512], BF16) for q in range(nkq)] for i in range(nslot)]
        yT = cx.sb(st, "yT", [128, 16, 512], F32)
        xt = cx.sb(st, "xres", [128, 16, 512], F32)
        sq = cx.sb(st, "ysq", [128, 16, 512], BF16)
        rstd = cx.sb(st, "yrstd", [128, 512], F32)
        wsl = [cx.sb(st, f"w2sl{i}", [128, 16, 256], BF16) for i in range(3)]
        ones = consts["ones"]
        inv = inT_d.rearrange("(k p) s -> p k s", p=128)
        xsv = x_src.rearrange("(k p) s -> p k s", p=128)
        xdv = x_dst.rearrange("(k p) s -> p k s", p=128)
        wi = 0
        NT = SL // 512

        def load_in(tt):
            for q in range(nkq):
                b = inT[tt % nslot][q]
                for k0 in range(0, 16, 8):
                    sch.dma("sp", b.ap[:, k0:k0 + 8, :], inv[:, q * 16 + k0:q * 16 + k0 + 8, tt * 512:(tt + 1) * 512], b,
                            reads=[in_buf], writes=[b])

        def load_x(tt):
            for k0 in range(0, 16, 8):
                sch.dma("sp", xt.ap[:, k0:k0 + 8, :], xsv[:, k0:k0 + 8, tt * 512:(tt + 1) * 512], xt,
                        reads=[x_src_buf], writes=[xt])

        def epilogue(tt):
            rms_scale_tile(cx, st, yT, sq, rstd, ones, 16, 512, D)
            for k in range(16):
                sch.op("dve", lambda e, k=k: e.scalar_tensor_tensor(
                    yT.ap[:, k, :], yT.ap[:, k, :], g_col.ap[:, k:k + 1], rstd.ap[:, :], ALU.mult, ALU.mult),
                    reads=[yT, rstd, g_col], writes=[yT], selfdep=(k == 0))
            for k in range(16):
                sch.op("dve", lambda e, k=k: e.tensor_tensor(xt.ap[:, k, :], xt.ap[:, k, :], yT.ap[:, k, :], ALU.add),
                       reads=[yT, xt], writes=[xt])
            for k0 in range(0, 16, 8):
                sch.dma("sp", xdv[:, k0:k0 + 8, tt * 512:(tt + 1) * 512], xt.ap[:, k0:k0 + 8, :], xt,
                        reads=[xt], writes=[x_dst_buf])
            if tt + 1 < NT:
                load_x(tt + 1)

        load_in(0)
        load_x(0)
        for tt in range(NT):
            i_ = inT[tt % nslot]
            for cg in range(8):
                banks = [cx.bank(), cx.bank()]
                for kq in range(nkq):
                    w_ = wsl[wi % 3]
                    wi += 1
                    slab = cg * nkq + kq
                    if tt == 0:
                        wv = w_view_fn(kq * 2048, (kq + 1) * 2048, cg * 256, (cg + 1) * 256).rearrange("(k p) c -> p k c", p=128)
                        for k0 in range(0, 16, 8):
                            sch.dma("pool", w_.ap[:, k0:k0 + 8, :], wv[:, k0:k0 + 8, :], w_, writes=[w_])
                        sch.dma("sp", wc_d[slab], w_.ap[:, :, :], w_, reads=[w_], writes=[wc_buf])
                    else:
                        sch.dma("pool", w_.ap[:, :, :], wc_d[slab], w_, reads=[wc_buf], writes=[w_])
                    ib = i_[kq]
                    def mm(e, kq=kq, w_=w_, banks=banks, ib=ib):
                        ins = None
                        for cs in range(2):
                            for k in range(16):
                                ins = e.matmul(banks[cs].ap[:, :], w_.ap[:, k, cs * 128:(cs + 1) * 128],
                                               ib.ap[:, k, :], start=(kq == 0 and k == 0),
                                               stop=(kq == nkq - 1 and k == 15))
                        return ins
                    sch.op("pe", mm, reads=[ib, w_], writes=banks)
                if cg == 0 and tt > 0:
                    epilogue(tt - 1)
                for cs in range(2):
                    kk = cg * 2 + cs
                    if cs == 0:
                        sch.op("act", lambda e, b=banks[cs], kk=kk: e.activation(yT.ap[:, kk, :], b.ap[:, :], AF.Copy),
                               reads=[banks[cs]], writes=[yT])
                    else:
                        sch.op("dve", lambda e, b=banks[cs], kk=kk: e.tensor_copy(yT.ap[:, kk, :], b.ap[:, :]),
                               reads=[banks[cs]], writes=[yT])
            if tt + 1 < NT:
                load_in(tt + 1)
        epilogue(NT - 1)
        cx.end_phase()


def phase_conv(cx, pda, pdg, b_da, b_dg, hgat, b_hg, flag, cw, cb, cn, l, mixT, mix_buf, consts):
    sch = cx.sch
    with contextlib.ExitStack() as st:
        hb = [cx.sb(st, f"cv_h{i}", [128, 30 + SL], F32) for i in range(2)]
        gt = [cx.sb(st, f"cv_g{i}", [128, SL], F32) for i in range(2)]
        hg = [cx.sb(st, f"cv_hg{i}", [128, 32], F32) for i in range(2)]
        acc = [cx.sb(st, f"cv_acc{i}", [128, SL], F32) for i in range(4)]
        sq = cx.sb(st, "cv_sq", [128, 4, SL], BF16)
        rstd = cx.sb(st, "cv_rstd", [128, SL], F32)
        tmp = [cx.sb(st, f"cv_tmp{i}", [128, SL], F32) for i in range(2)]
        ob = [cx.sb(st, f"cv_ob{i}", [128, SL], BF16) for i in range(2)]
        ones = consts["ones"]
        for cc in range(4):
            h_, g_, hg_ = hb[cc % 2], gt[cc % 2], hg[cc % 2]
            eng = "dve"
            sch.dma("sp", h_.ap[:, 30:], pda[cc * 128:(cc + 1) * 128, :], h_, reads=[b_da], writes=[h_])
            sch.dma("sp", h_.ap[:, 0:30], hgat[cc * 128:(cc + 1) * 128, 2:32], h_, reads=[b_hg], writes=[h_])
            sch.dma("sp", hg_.ap[:, 0:30], hgat[GW + cc * 128:GW + (cc + 1) * 128, 2:32], hg_, reads=[b_hg], writes=[hg_])
            sch.dma("sp", g_.ap[:, :], pdg[cc * 128:(cc + 1) * 128, :], g_, reads=[b_dg], writes=[g_])
            sch.op("act", lambda e, g_=g_: e.activation(g_.ap[:, :], g_.ap[:, :], AF.Sigmoid), reads=[g_], writes=[g_])
            sch.op("act", lambda e, hg_=hg_: e.activation(hg_.ap[:, 0:30], hg_.ap[:, 0:30], AF.Sigmoid), reads=[hg_], writes=[hg_])
            sch.op(eng, lambda e, h_=h_, g_=g_: e.tensor_tensor(h_.ap[:, 30:], h_.ap[:, 30:], g_.ap[:, :], ALU.mult),
                   reads=[h_, g_], writes=[h_])
            sch.op(eng, lambda e, h_=h_, hg_=hg_: e.scalar_tensor_tensor(
                h_.ap[:, 0:30], h_.ap[:, 0:30], flag.ap[:, 0:1], hg_.ap[:, 0:30], ALU.mult, ALU.mult),
                reads=[h_, hg_, flag], writes=[h_])
            a_ = acc[cc]
            wbase = (l * 4 + cc) * CONVW
            bcol = cb.ap[:, l * 4 + cc:l * 4 + cc + 1]
            def conv(e, h_=h_, a_=a_, wbase=wbase, bcol=bcol):
                ins = e.tensor_scalar(a_.ap[:, :], h_.ap[:, 0:SL], cw.ap[:, wbase:wbase + 1], bcol, ALU.mult, ALU.add)
                for j in range(1, CONVW):
                    ins = e.scalar_tensor_tensor(a_.ap[:, :], h_.ap[:, j:j + SL], cw.ap[:, wbase + j:wbase + j + 1],
                                                 a_.ap[:, :], ALU.mult, ALU.add)
                return ins
            sch.op(eng, conv, reads=[h_, cw, cb], writes=[a_], selfdep=True)
            sch.op("act", lambda e, a_=a_, cc=cc: e.activation(sq.ap[:, cc, :], a_.ap[:, :], AF.Square),
                   reads=[a_], writes=[sq])
        for tt in range(SL // 512):
            bank = cx.bank()
            def mm(e, tt=tt, bank=bank):
                ins = None
                for cc in range(4):
                    ins = e.matmul(bank.ap[:, :], ones.ap[:, :], sq.ap[:, cc, tt * 512:(tt + 1) * 512],
                                   start=(cc == 0), stop=(cc == 3))
                return ins
            sch.op("pe", mm, reads=[sq, ones], writes=[bank])
            sch.op("dve", lambda e, tt=tt, bank=bank: e.tensor_scalar(
                rstd.ap[:, tt * 512:(tt + 1) * 512], bank.ap[:, :], 1.0 / GW, EPS, ALU.mult, ALU.add),
                reads=[bank], writes=[rstd])
        sch.op("act", lambda e: e.activation(rstd.ap[:, :], rstd.ap[:, :], AF.Sqrt), reads=[rstd], writes=[rstd])
        sch.op("dve", lambda e: e.reciprocal(rstd.ap[:, :], rstd.ap[:, :]), reads=[rstd], writes=[rstd])
        for cc in range(4):
            t_, o_ = tmp[cc % 2], ob[cc % 2]
            sch.op("dve", lambda e, cc=cc, t_=t_: e.scalar_tensor_tensor(
                t_.ap[:, :], acc[cc].ap[:, :], cn.ap[:, l * 4 + cc:l * 4 + cc + 1], rstd.ap[:, :], ALU.mult, ALU.mult),
                reads=[acc[cc], rstd, cn], writes=[t_], selfdep=True)
            sch.op("act", lambda e, t_=t_, o_=o_: e.activation(o_.ap[:, :], t_.ap[:, :], AF.Silu), reads=[t_], writes=[o_])
            sch.dma("sp", mixT[1536 + cc * 128:1536 + (cc + 1) * 128, :], o_.ap[:, :], o_, reads=[o_], writes=[mix_buf])
        cx.end_phase()


def phase_gmlp(cx, pu, pv, b_u, b_v, gw_d, gbT, l, mixT, mix_buf, consts):
    sch = cx.sch
    with contextlib.ExitStack() as st:
        wraw = cx.sb(st, "gm_wraw", [128, 8, 128], F32)
        WT = cx.sb(st, "gm_WT", [128, 8, 128], BF16)
        bfull = cx.sb(st, "gm_bfull", [128, GW], F32)
        ut = [cx.sb(st, f"gm_u{i}", [128, GW], F32) for i in range(2)]
        vt = [cx.sb(st, f"gm_v{i}", [128, GW], F32) for i in range(2)]
        vn = [cx.sb(st, f"gm_vn{i}", [128, GW], BF16) for i in range(2)]
        t32 = [cx.sb(st, f"gm_t{i}", [128, GW], F32) for i in range(2)]
        oa = [cx.sb(st, f"gm_oa{i}", [128, GW], BF16) for i in range(2)]
        oT = [cx.sb(st, f"gm_oT{i}", [128, 4, 128], BF16) for i in range(2)]
        st1 = [cx.sb(st, f"gm_s{i}", [128, 4], F32) for i in range(2)]
        identf, identb, tri = consts["identf"], consts["identb"], consts["tri"]
        pbf = consts["psum_bf"]
        sch.dma("sp", wraw.ap[:, :, :], gw_d[l], wraw, writes=[wraw])
        for g in range(8):
            sch.op("dve", lambda e, g=g: e.tensor_tensor(WT.ap[:, g, :], wraw.ap[:, g, :], tri.ap[:, :], ALU.mult),
                   reads=[wraw, tri], writes=[WT])
        sch.op("dve", lambda e: e.memset(bfull.ap[:, :], 0.0), writes=[bfull])
        for g in range(8):
            sch.op("dve", lambda e, g=g: e.tensor_scalar(
                bfull.ap[:, g * 64:(g + 1) * 64], bfull.ap[:, g * 64:(g + 1) * 64], gbT.ap[:, l * 8 + g:l * 8 + g + 1], 0.0, ALU.add, ALU.add),
                reads=[bfull, gbT], writes=[bfull], selfdep=(g == 0))
        def ld(c):
            sch.dma("sp", ut[c % 2].ap[:, :], pu[c * 128:(c + 1) * 128, :], ut[c % 2], reads=[b_u], writes=[ut[c % 2]])
            sch.dma("sp", vt[c % 2].ap[:, :], pv[c * 128:(c + 1) * 128, :], vt[c % 2], reads=[b_v], writes=[vt[c % 2]])
        ld(0)
        for c in range(NBL):
            u_, v_, vn_, t_, oa_, oT_, s_ = ut[c % 2], vt[c % 2], vn[c % 2], t32[c % 2], oa[c % 2], oT[c % 2], st1[c % 2]
            if c + 1 < NBL:
                ld(c + 1)
            sch.op("act", lambda e, u_=u_: e.activation(u_.ap[:, :], u_.ap[:, :], AF.Gelu_apprx_tanh), reads=[u_], writes=[u_])
            sch.op("act", lambda e, v_=v_: e.activation(v_.ap[:, :], v_.ap[:, :], AF.Gelu_apprx_tanh), reads=[v_], writes=[v_])
            def rs0(e, v_=v_, s_=s_):
                e.memset(s_.ap[:, 1:2], 0.0)
                return e.reduce_sum(s_.ap[:, 0:1], v_.ap[:, :], AX.X)
            sch.op("dve", rs0, reads=[v_], writes=[s_])
            sch.op("act", lambda e, s_=s_: e.activation(s_.ap[:, 0:1], s_.ap[:, 0:1], AF.Copy, scale=-1.0 / GW), reads=[s_], writes=[s_])
            sch.op("dve", lambda e, v_=v_, s_=s_: e.tensor_scalar(v_.ap[:, :], v_.ap[:, :], s_.ap[:, 0:1], 0.0, ALU.add, ALU.add),
                   reads=[v_, s_], writes=[v_])
            sch.op("act", lambda e, v_=v_, s_=s_, t_=t_: e.activation(t_.ap[:, :], v_.ap[:, :], AF.Square, accum_out=s_.ap[:, 1:2]),
                   reads=[v_], writes=[t_, s_])
            sch.op("dve", lambda e, s_=s_: e.tensor_scalar(s_.ap[:, 1:2], s_.ap[:, 1:2], 1.0 / GW, EPS, ALU.mult, ALU.add), reads=[s_], writes=[s_])
            sch.op("act", lambda e, s_=s_: e.activation(s_.ap[:, 1:2], s_.ap[:, 1:2], AF.Sqrt), reads=[s_], writes=[s_])
            sch.op("dve", lambda e, s_=s_: e.reciprocal(s_.ap[:, 2:3], s_.ap[:, 1:2]), reads=[s_], writes=[s_])
            sch.op("act", lambda e, v_=v_, s_=s_, vn_=vn_: e.activation(vn_.ap[:, :], v_.ap[:, :], AF.Copy, scale=s_.ap[:, 2:3]),
                   reads=[v_, s_], writes=[vn_])
            bank = cx.bank()
            def mm(e, vn_=vn_, bank=bank):
                ins = None
                for g in range(8):
                    ins = e.matmul(bank.ap[:, g * 64:(g + 1) * 64], WT.ap[:, g, :], vn_.ap[:, g * 64:(g + 1) * 64],
                                   start=True, stop=True)
                return ins
            sch.op("pe", mm, reads=[WT, vn_], writes=[bank])
            seq(sch, "dve", [
                (lambda e, bank=bank, t_=t_: e.tensor_tensor(t_.ap[:, :], bank.ap[:, :], bfull.ap[:, :], ALU.add), [bank, bfull], [t_]),
                (lambda e, t_=t_, u_=u_, oa_=oa_: e.tensor_tensor(oa_.ap[:, :], t_.ap[:, :], u_.ap[:, :], ALU.mult), [t_, u_], [oa_]),
            ])
            if c == 0 and cx.dbg is not None:
                for nm, b_ in [("u", u_), ("vn", vn_), ("t", t_), ("oa", oa_), ("bfull", bfull), ("v", v_)]:
                    sch.dma("sp", cx.dbg[nm], b_.ap[:, :], b_, reads=[b_], writes=[cx.dbgb])
            tb = cx.bank()
            def tr(e, oa_=oa_, tb=tb):
                ins = None
                for j in range(4):
                    ins = e.matmul(tb.ap[:, j * 128:(j + 1) * 128], oa_.ap[:, j * 128:(j + 1) * 128], identb.ap[:, :],
                                   start=True, stop=True)
                return ins
            sch.op("pe", tr, reads=[oa_, identb], writes=[tb])
            sch.op("act", lambda e, oT_=oT_, tb=tb: e.activation(oT_.ap[:, :, :], tb.ap[:, :].rearrange("p (j t) -> p j t", j=4), AF.Copy),
                   reads=[tb], writes=[oT_])
            sch.dma("sp", mixT[0:GW, c * 128:(c + 1) * 128].rearrange("(j p) t -> p j t", p=128), oT_.ap[:, :, :], oT_,
                    reads=[oT_], writes=[mix_buf])
        cx.end_phase()


def phase_attn(cx, kind, qT_d, kT_d, kprev_d, v_d, vprev_d, b_q, b_k, b_kg, b_v, b_vg, flag, l, mixT, mix_buf, mix_row0,
               consts, lam_bufs=None):
    sch = cx.sch
    diff = kind == "diff"
    E = 128 if diff else 64
    EA = E + 1
    alibi, ctab, tri, identb, pbf = consts["alibi"], consts["ctab"], consts["tri"], consts["identb"], consts["psum_bf"]
    DIFF_IDX = [0, 3, 6, 9]
    DIL_IDX = [1, 2, 4, 5, 7, 8, 10, 11]
    banks = cx.psum
    with contextlib.ExitStack() as st:
        KT = [cx.sb(st, f"at_K{i}", [128, S], BF16) for i in range(2)]
        QT = [cx.sb(st, f"at_Q{i}", [128, SL], BF16) for i in range(2)]
        nV = 1 if diff else 2
        VA = [[cx.sb(st, f"at_V{i}_{m}", [128, NB, EA], BF16) for m in range(nV)] for i in range(2)]
        PT = [cx.sb(st, f"at_P{i}", [128, 512], BF16) for i in range(4)]
        fin32 = [cx.sb(st, f"at_f{i}", [128, 2, 128], F32) for i in range(2)]
        fs = [cx.sb(st, f"at_s{i}", [128, 8], F32) for i in range(2)]
        ob = [cx.sb(st, f"at_ob{i}", [128, 128], BF16) for i in range(2)]
        oT = [cx.sb(st, f"at_oT{i}", [128, 512], BF16) for i in range(2)]
        for i in range(2):
            for m in range(nV):
                sch.op("dve", lambda e, i=i, m=m: e.memset(VA[i][m].ap[:, :, E:EA], 1.0), writes=[VA[i][m]])
        def acc_region(m, j):
            if diff:
                return m * 2 + j // 2, (j % 2) * EA
            return m, j * EA
        nacc = 4 if diff else 2
        accw = (2 if diff else 4) * EA
        asb = [cx.sb(st, f"at_acc{i}", [128, nacc, accw], F32) for i in range(2)]
        stb = [banks[4], banks[5], pbf]
        trb = banks[6]
        sti = 0
        pti = 0
        fi = 0
        def load_u(u):
            K_, Q_, V_ = KT[u % 2], QT[u % 2], VA[u % 2]
            sch.dma("sp", K_.ap[:, 0:SL], kprev_d[u * 128:(u + 1) * 128, :], K_, reads=[b_kg], writes=[K_])
            sch.dma("sp", K_.ap[:, SL:S], kT_d[u * 128:(u + 1) * 128, :], K_, reads=[b_k], writes=[K_])
            sch.dma("sp", Q_.ap[:, :], qT_d[u * 128:(u + 1) * 128, :], Q_, reads=[b_q], writes=[Q_])
            for m in range(nV):
                c0 = u * 128 + (0 if diff else m * 64)
                sch.dma("sp", V_[m].ap[:, 0:NBL, 0:E], vprev_d[:, c0:c0 + E].rearrange("(n p) e -> p n e", p=128), V_[m],
                        reads=[b_vg], writes=[V_[m]])
                sch.dma("sp", V_[m].ap[:, NBL:NB, 0:E], v_d[:, c0:c0 + E].rearrange("(n p) e -> p n e", p=128), V_[m],
                        reads=[b_v], writes=[V_[m]])
        load_u(0)
        for u in range(4):
            K_, Q_, V_ = KT[u % 2], QT[u % 2], VA[u % 2]
            for m in range(nV):
                sch.op("dve", lambda e, Vm=V_[m]: e.tensor_scalar(
                    Vm.ap[:, 0:NBL, :], Vm.ap[:, 0:NBL, :], flag.ap[:, 0:1], 0.0, ALU.mult, ALU.add),
                    reads=[V_[m], flag], writes=[V_[m]])
            if u + 1 < 4:
                load_u(u + 1)
            for qg in (2, 3):
                steps = []
                for m in range(2):
                    hidx = DIFF_IDX[u] if diff else DIL_IDX[u * 2 + m]
                    slope = 2.0 ** (-8.0 * (hidx + 1) / N_ALIBI)
                    fine = slope > 0.26
                    for kb in range(qg * 4 + 4):
                        steps.append((m, hidx, fine, kb))
                pend = []
                started = set()
                def emit_pv(stp, P_):
                    m, hidx, fine, kb = stp
                    jlo = max(0, kb - qg * 4)
                    Vm = V_[0] if diff else V_[m]
                    for j in range(jlo, 4):
                        bi, c0 = acc_region(m, j)
                        first = bi not in started
                        started.add(bi)
                        ab = banks[bi]
                        sch.op("pe", lambda e, ab=ab, c0=c0, P_=P_, Vm=Vm, j=j, kb=kb, first=first, qg=qg: e.matmul(
                            ab.ap[:, c0:c0 + EA], P_.ap[:, j * 128:(j + 1) * 128], Vm.ap[:, kb, :], start=first,
                            stop=(kb == qg * 4 + j), skip_group_check=True),
                            reads=[P_, Vm], writes=[ab])
                for stp in steps:
                    m, hidx, fine, kb = stp
                    jlo = max(0, kb - qg * 4)
                    c_lo = jlo * 128
                    sb_ = stb[sti % 3]
                    sti += 1
                    P_ = PT[pti % 4]
                    pti += 1
                    sch.op("pe", lambda e, sb_=sb_, K_=K_, Q_=Q_, m=m, kb=kb, c_lo=c_lo, qg=qg: e.matmul(
                        sb_.ap[:, c_lo:512], K_.ap[m * 64:(m + 1) * 64, kb * 128:(kb + 1) * 128],
                        Q_.ap[m * 64:(m + 1) * 64, (qg - 2) * 512 + c_lo:(qg - 1) * 512], start=True, stop=True),
                        reads=[K_, Q_], writes=[sb_])
                    if fine:
                        def ex(e, sb_=sb_, P_=P_, hidx=hidx, kb=kb, jlo=jlo, qg=qg):
                            ins = None
                            for j in range(jlo, 4):
                                v = 16 + (kb - (qg * 4 + j) + 15)
                                ins = e.activation(P_.ap[:, j * 128:(j + 1) * 128], sb_.ap[:, j * 128:(j + 1) * 128], AF.Exp,
                                                   bias=alibi.ap[:, hidx * 32 + v:hidx * 32 + v + 1])
                            return ins
                    else:
                        def ex(e, sb_=sb_, P_=P_, hidx=hidx, kb=kb, c_lo=c_lo, qg=qg):
                            v = kb - 4 * qg - 2 + 14
                            return e.activation(P_.ap[:, c_lo:512], sb_.ap[:, c_lo:512], AF.Exp,
                                                bias=alibi.ap[:, hidx * 32 + v:hidx * 32 + v + 1])
                    sch.op("act", ex, reads=[sb_, alibi], writes=[P_])
                    if diff:
                        if kb >= qg * 4:
                            sch.op("dve", lambda e, P_=P_, jlo=jlo: e.tensor_tensor(
                                P_.ap[:, jlo * 128:(jlo + 1) * 128], P_.ap[:, jlo * 128:(jlo + 1) * 128], tri.ap[:, :], ALU.mult),
                                reads=[P_, tri], writes=[P_])
                    else:
                        u0 = qg * 512 + c_lo - kb * 128
                        sch.op("dve", lambda e, P_=P_, c_lo=c_lo, u0=u0: e.tensor_tensor(
                            P_.ap[:, c_lo:512], P_.ap[:, c_lo:512], ctab.ap[:, u0:u0 + 512 - c_lo], ALU.mult),
                            reads=[P_, ctab], writes=[P_])
                    pend.append((stp, P_))
                    if len(pend) > 3:
                        emit_pv(*pend.pop(0))
                while pend:
                    emit_pv(*pend.pop(0))
                asb_ = asb[(u * 4 + qg) % 2]
                for bi in range(nacc):
                    sch.op("dve", lambda e, bi=bi, asb_=asb_: e.tensor_copy(asb_.ap[:, bi, :], banks[bi].ap[:, 0:accw]),
                           reads=[banks[bi]], writes=[asb_])
                oT_ = oT[(u * 4 + qg) % 2]
                for j in range(4):
                    f_, s_, ob_ = fin32[fi % 2], fs[fi % 2], ob[fi % 2]
                    fi += 1
                    b0, c00 = acc_region(0, j)
                    b1, c01 = acc_region(1, j)
                    a0 = Buf("a0v", asb_.ap[:, b0, c00:c00 + EA])
                    a1 = Buf("a1v", asb_.ap[:, b1, c01:c01 + EA])
                    if diff:
                        neg_lam, gsub = lam_bufs
                        def recs(e, a0=a0, a1=a1, s_=s_):
                            e.memset(s_.ap[:, 2:3], 0.0)
                            e.reciprocal(s_.ap[:, 0:1], a0.ap[:, E:EA])
                            return e.reciprocal(s_.ap[:, 1:2], a1.ap[:, E:EA])
                        sch.op("dve", recs, reads=[asb_], writes=[s_], selfdep=True)
                        def nrm(e, a0=a0, a1=a1, s_=s_, f_=f_):
                            e.activation(f_.ap[:, 0, :], a0.ap[:, 0:E], AF.Copy, scale=s_.ap[:, 0:1])
                            return e.activation(f_.ap[:, 1, :], a1.ap[:, 0:E], AF.Copy, scale=s_.ap[:, 1:2])
                        sch.op("act", nrm, reads=[asb_, s_], writes=[f_])
                        sch.op("dve", lambda e, f_=f_: e.scalar_tensor_tensor(f_.ap[:, 0, :], f_.ap[:, 1, :], neg_lam.ap[:, 0:1], f_.ap[:, 0, :],
                                                                              ALU.mult, ALU.add), reads=[f_, neg_lam], writes=[f_])
                        sch.op("act", lambda e, f_=f_, s_=s_: e.activation(f_.ap[:, 1, :], f_.ap[:, 0, :], AF.Square, accum_out=s_.ap[:, 2:3]),
                               reads=[f_], writes=[f_, s_])
                        sch.op("dve", lambda e, s_=s_: e.tensor_scalar(s_.ap[:, 2:3], s_.ap[:, 2:3], 1.0 / 128, EPS, ALU.mult, ALU.add), reads=[s_], writes=[s_])
                        sch.op("act", lambda e, s_=s_: e.activation(s_.ap[:, 2:3], s_.ap[:, 2:3], AF.Sqrt), reads=[s_], writes=[s_])
                        sch.op("dve", lambda e, s_=s_: e.reciprocal(s_.ap[:, 3:4], s_.ap[:, 2:3]), reads=[s_], writes=[s_])
                        sch.op("act", lambda e, f_=f_, s_=s_: e.activation(f_.ap[:, 1, :], f_.ap[:, 0, :], AF.Copy, scale=s_.ap[:, 3:4]),
                               reads=[f_, s_], writes=[f_])
                        sch.op("dve", lambda e, f_=f_, ob_=ob_: e.tensor_tensor(ob_.ap[:, :], f_.ap[:, 1, :], gsub.ap[:, :], ALU.mult),
                               reads=[f_, gsub], writes=[ob_])
                    else:
                        def recs(e, a0=a0, a1=a1, s_=s_):
                            e.memset(s_.ap[:, 2:3], 0.0)
                            e.reciprocal(s_.ap[:, 0:1], a0.ap[:, E:EA])
                            return e.reciprocal(s_.ap[:, 1:2], a1.ap[:, E:EA])
                        sch.op("dve", recs, reads=[asb_], writes=[s_], selfdep=True)
                        def nrm(e, a0=a0, a1=a1, s_=s_, ob_=ob_):
                            e.activation(ob_.ap[:, 0:64], a0.ap[:, 0:E], AF.Copy, scale=s_.ap[:, 0:1])
                            return e.activation(ob_.ap[:, 64:128], a1.ap[:, 0:E], AF.Copy, scale=s_.ap[:, 1:2])
                        sch.op("act", nrm, reads=[asb_, s_], writes=[ob_])
                    sch.op("pe", lambda e, ob_=ob_, j=j: e.matmul(trb.ap[:, j * 128:(j + 1) * 128], ob_.ap[:, :], identb.ap[:, :],
                                                                  start=True, stop=True),
                           reads=[ob_, identb], writes=[trb])
                sch.op("act", lambda e, oT_=oT_: e.activation(oT_.ap[:, :], trb.ap[:, :], AF.Copy), reads=[trb], writes=[oT_])
                sch.dma("sp", mixT[mix_row0 + u * 128:mix_row0 + (u + 1) * 128, (qg - 2) * 512:(qg - 1) * 512], oT_.ap[:, :], oT_,
                        reads=[oT_], writes=[mix_buf])
        cx.end_phase()


def build(nlayers, first_layer=0, debug=False):
    nc = bass.Bass("TRN2", target_bir_lowering=False)
    skind = "ExternalOutput" if debug else "Internal"
    L = DEPTH
    ein = lambda n, shp, dt=F32: nc.dram_tensor(n, list(shp), dt, kind="ExternalInput").ap()
    scr = lambda n, shp, dt: nc.dram_tensor(n, list(shp), dt, kind=skind).ap()
    xT_in = ein("xT", [D, SL])
    flag_in = ein("flag", [128, 1])
    w_in = ein("w_in", [L, D, INW])
    w_out = ein("w_out", [L, D, D])
    w_ff1 = ein("w_ff1", [L, D, DFF])
    w_ff2 = ein("w_ff2", [L, DFF, D])
    gmlp_w = ein("gmlp_w", [L, 128, 8, 128])
    gcols = ein("gcols", [128, 4 * L * 16])
    c_f32 = ein("c_f32", [128, 384 + 128])
    c_bf = ein("c_bf", [128, 128 + 128 + 128 + S])
    p_cols = ein("p_cols", [128, L * 4 * CONVW + L * 4 + L * 4 + L * 8])
    p_rep = ein("p_rep", [128, L * 256 + L * 128])
    outT = nc.dram_tensor("outT", [D, SL], F32, kind="ExternalOutput").ap()
    pu, pv = scr("s_u", [SL, GW], F32), scr("s_v", [SL, GW], F32)
    pbq, pcq = scr("s_bq", [GW, SL], BF16), scr("s_cq", [GW, SL], BF16)
    ks_t = nc.dram_tensor("s_ksend", [2 * GW, SL], BF16, kind="Internal")
    kg_t = nc.dram_tensor("s_kgat", [4 * GW, SL], BF16, kind="Internal")
    vs_t = nc.dram_tensor("s_vsend", [2 * SL, GW], BF16, kind="Internal")
    vg_t = nc.dram_tensor("s_vgat", [4 * SL, GW], BF16, kind="Internal")
    hs_t = nc.dram_tensor("s_hsend", [2 * GW, 32], F32, kind="Internal")
    hg_t = nc.dram_tensor("s_hgat", [4 * GW, 32], F32, kind="Internal")
    ksend, kgat, vsend, vgat, hsend, hgat = (t.ap() for t in (ks_t, kg_t, vs_t, vg_t, hs_t, hg_t))
    pbk, pck = ksend[0:GW, :], ksend[GW:2 * GW, :]
    pbv, pcv = vsend[0:SL, :], vsend[SL:2 * SL, :]
    pda, pdg = scr("s_da", [GW, SL], F32), scr("s_dg", [GW, SL], F32)
    mixT = scr("s_mixT", [D, SL], BF16)
    fT = scr("s_fT", [DFF, SL], BF16)
    xA, xB = scr("s_xA", [D, SL], F32), scr("s_xB", [D, SL], F32)
    wcache = scr("s_wc", [32, 128, 16, 256], BF16)

    with contextlib.ExitStack() as stack:
        sch = Sched(nc, stack)
        cx = Ctx(nc, sch, stack)
        for i in range(7):
            t = stack.enter_context(nc.psum_tensor(f"ps{i}", [128, 512], F32))
            cx.psum.append(Buf(f"ps{i}", t))
        if debug:
            cx.dbg = {n: nc.dram_tensor("dbg_" + n, shp, dt, kind="ExternalOutput").ap() for n, shp, dt in [
                ("u", [128, 512], F32), ("vn", [128, 512], BF16), ("t", [128, 512], F32), ("oa", [128, 512], BF16),
                ("bfull", [128, 512], F32), ("v", [128, 512], F32), ("WT", [128, 8, 128], BF16)]}
            cx.dbgb = dbuf("dbgb")
        consts = {}
        pbf_t = stack.enter_context(nc.psum_tensor("psbf", [128, 512], F32))
        consts["psum_bf"] = Buf("psbf", pbf_t)
        cf = cx.sb(stack, "c_f32", [128, 512], F32)
        cb_ = cx.sb(stack, "c_bf", [128, 384 + S], BF16)
        pc = cx.sb(stack, "p_cols", [128, L * 4 * CONVW + L * 16], F32)
        pr = cx.sb(stack, "p_rep", [128, L * 384], F32)
        gc = cx.sb(stack, "gc", [128, 4 * L * 16], F32)
        flag = cx.sb(stack, "flag", [128, 1], F32)
        ccs = [cx.sb(stack, f"ccslot{i}", [128, 1], F32) for i in range(4)]
        sch.dma("sp", cf.ap[:, :], c_f32[:, :], cf, writes=[cf])
        sch.dma("pool", cb_.ap[:, :], c_bf[:, :], cb_, writes=[cb_])
        sch.dma("sp", pc.ap[:, :], p_cols[:, :], pc, writes=[pc])
        sch.dma("sp", pr.ap[:, :], p_rep[:, :], pr, writes=[pr])
        sch.dma("sp", gc.ap[:, :], gcols[:, :], gc, writes=[gc])
        sch.dma("sp", flag.ap[:, :], flag_in[:, :], flag, writes=[flag])
        def view(parent, ap):
            b = Buf(parent.name + "_v", ap)
            b.w = parent.w
            return b
        consts["alibi"] = view(cf, cf.ap[:, 0:384])
        consts["identf"] = view(cf, cf.ap[:, 384:512])
        consts["ones"] = view(cb_, cb_.ap[:, 0:128])
        consts["identb"] = view(cb_, cb_.ap[:, 128:256])
        consts["tri"] = view(cb_, cb_.ap[:, 256:384])
        consts["ctab"] = view(cb_, cb_.ap[:, 384:384 + S])
        o1 = L * 4 * CONVW
        cw = view(pc, pc.ap[:, 0:o1])
        cbias = view(pc, pc.ap[:, o1:o1 + L * 4])
        cnorm = view(pc, pc.ap[:, o1 + L * 4:o1 + L * 8])
        gbT = view(pc, pc.ap[:, o1 + L * 8:o1 + L * 16])
        lamw = cx.sb(stack, "lamw", [128, 16], F32)
        lamt = cx.sb(stack, "lamt", [128, 128], F32)
        lamt2 = cx.sb(stack, "lamt2", [128, 128], F32)
        gsub = cx.sb(stack, "gsub", [128, 128], F32)
        neg_lam = cx.sb(stack, "neg_lam", [128, 1], F32)
        cx.cur = []

        bx = {n: dbuf("b_" + n) for n in ["xin", "xA", "xB", "out", "u", "v", "bq", "bk", "bv", "cq", "ck", "cv", "da", "dg", "mix", "f", "wc", "kg", "vg", "hs", "hg"]}
        x_cur, x_cur_b = xT_in, bx["xin"]
        for li in range(nlayers):
            l = first_layer + li
            last = li == nlayers - 1
            g = lambda which: view(gc, gc.ap[:, (which * L + l) * 16:(which * L + l) * 16 + 16])
            outs = [
                dict(mode="tok", dst=pu, dt=F32, dbuf=bx["u"]),
                dict(mode="tok", dst=pv, dt=F32, dbuf=bx["v"]),
                dict(mode="feat", dst=pbq, dt=BF16, scale=0.125, dbuf=bx["bq"]),
                dict(mode="feat", dst=pbk, dt=BF16, dbuf=bx["bk"]),
                dict(mode="tok", dst=pbv, dt=BF16, dbuf=bx["bv"]),
                dict(mode="feat", dst=pcq, dt=BF16, scale=0.125, dbuf=bx["cq"]),
                dict(mode="feat", dst=pck, dt=BF16, dbuf=bx["ck"]),
                dict(mode="tok", dst=pcv, dt=BF16, dbuf=bx["cv"]),
                dict(mode="feat", dst=pda, dt=F32, dbuf=bx["da"]),
                dict(mode="feat", dst=pdg, dt=F32, dbuf=bx["dg"]),
            ]
            order = [3, 4, 6, 7, 8, 9, 0, 1, 2, 5]
            def gather_kv():
                sch.cc(ks_t.ap().opt(), kg_t.ap().opt(), ccs[0], reads=[bx["bk"], bx["ck"]], writes=[bx["kg"]])
                sch.cc(vs_t.ap().opt(), vg_t.ap().opt(), ccs[1], reads=[bx["bv"], bx["cv"]], writes=[bx["vg"]])
            def gather_halo():
                sch.dma("sp", hsend[0:GW, :], pda[:, SL - 32:SL], ccs[3], reads=[bx["da"]], writes=[bx["hs"]])
                sch.dma("sp", hsend[GW:2 * GW, :], pdg[:, SL - 32:SL], ccs[3], reads=[bx["dg"]], writes=[bx["hs"]])
                sch.cc(hs_t.ap().opt(), hg_t.ap().opt(), ccs[2], reads=[bx["hs"]], writes=[bx["hg"]])
            phase_norm_proj(cx, x_cur, x_cur_b, g(0), lambda i, l=l: w_in[l, :, order[i] * 512:(order[i] + 1) * 512], 10,
                            lambda i: outs[order[i]], consts, post_load={6: gather_kv, 8: gather_halo})
            lam_init = 0.8 - 0.6 * math.exp(-0.3 * l)
            lp = pr.ap[:, l * 256:(l + 1) * 256]
            def lam1(e, lp=lp):
                e.memset(lamw.ap[:, 0:2], 0.0)
                e.tensor_tensor(lamt.ap[:, 0:64], lp[:, 0:64], lp[:, 64:128], ALU.mult)
                return e.tensor_tensor(lamt.ap[:, 64:128], lp[:, 128:192], lp[:, 192:256], ALU.mult)
            sch.op("dve", lam1, reads=[pr], writes=[lamt, lamw])
            def lam2(e):
                e.activation(lamt2.ap[:, 0:64], lamt.ap[:, 0:64], AF.Copy, accum_out=lamw.ap[:, 0:1])
                return e.activation(lamt2.ap[:, 64:128], lamt.ap[:, 64:128], AF.Copy, accum_out=lamw.ap[:, 1:2])
            sch.op("act", lam2, reads=[lamt], writes=[lamt2, lamw])
            sch.op("dve", lambda e: e.tensor_copy(lamw.ap[:, 2:4], lamw.ap[:, 0:2]), reads=[lamw], writes=[lamw])
            sch.op("act", lambda e: e.activation(lamw.ap[:, 4:6], lamw.ap[:, 2:4], AF.Exp), reads=[lamw], writes=[lamw])
            sch.op("dve", lambda e, lam_init=lam_init: e.scalar_tensor_tensor(
                neg_lam.ap[:, 0:1], lamw.ap[:, 5:6], -lam_init, lamw.ap[:, 4:5], ALU.add, ALU.subtract), reads=[lamw], writes=[neg_lam])
            sch.op("dve", lambda e, l=l, lam_init=lam_init: e.tensor_scalar(
                gsub.ap[:, :], pr.ap[:, L * 256 + l * 128:L * 256 + (l + 1) * 128], 1.0 - lam_init, None, ALU.mult), reads=[pr], writes=[gsub])
            phase_gmlp(cx, pu, pv, bx["u"], bx["v"], gmlp_w, gbT, l, mixT, bx["mix"], consts)
            phase_attn(cx, "diff", pbq, pbk, kgat[0:GW, :], pbv, vgat[0:SL, :], bx["bq"], bx["bk"], bx["kg"], bx["bv"], bx["vg"],
                       flag, l, mixT, bx["mix"], 512, consts, lam_bufs=(neg_lam, gsub))
            phase_attn(cx, "dil", pcq, pck, kgat[GW:2 * GW, :], pcv, vgat[SL:2 * SL, :], bx["cq"], bx["ck"], bx["kg"], bx["cv"],
                       bx["vg"], flag, l, mixT, bx["mix"], 1024, consts)
            phase_conv(cx, pda, pdg, bx["da"], bx["dg"], hgat, bx["hg"], flag, cw, cbias, cnorm, l, mixT, bx["mix"], consts)
            phase_proj_norm_res(cx, mixT, bx["mix"], 16, lambda r0, r1, c0, c1, l=l: w_out[l, r0:r1, c0:c1], g(1),
                                x_cur, x_cur_b, xA, bx["xA"], consts, wcache, bx["wc"])
            fouts = [dict(mode="feat", dst=fT[i * 512:(i + 1) * 512, :], dt=BF16, relu2=True, dbuf=bx["f"]) for i in range(16)]
            phase_norm_proj(cx, xA, bx["xA"], g(2), lambda i, l=l: w_ff1[l, :, i * 512:(i + 1) * 512], 16,
                            lambda i: fouts[i], consts)
            xo, xob = (outT, bx["out"]) if last else (xB, bx["xB"])
            phase_proj_norm_res(cx, fT, bx["f"], 64, lambda r0, r1, c0, c1, l=l: w_ff2[l, r0:r1, c0:c1], g(3),
                                xA, bx["xA"], xo, xob, consts, wcache, bx["wc"])
            x_cur, x_cur_b = xB, bx["xB"]
        sch.barrier()
        sch.emit()
    return nc


def _count(d):
    return (d <= 128) * 1.0 + ((d % 4 == 0) & (d <= 512)) * 1.0 + ((d % 16 == 0) & (d <= 2048)) * 1.0


def host_consts():
    ki = np.arange(128, dtype=np.float64)[:, None]
    slopes = 2.0 ** (-8.0 * np.arange(1, N_ALIBI + 1, dtype=np.float64) / N_ALIBI)
    off = np.zeros(32)
    off[:16] = 128.0 * (np.arange(16) - 14)
    off[16:] = 128.0 * (np.arange(16) - 15) - 64.0
    alibi = (slopes[None, :, None] * (ki[:, :, None] + off[None, None, :])).reshape(128, 384)
    c_f32 = np.concatenate([alibi, np.eye(128)], axis=1).astype(np.float32)
    uu = np.arange(S)[None, :]
    dd = uu - np.arange(128)[:, None]
    ctab = np.where(dd >= 0, _count(np.maximum(dd, 0)), 0.0)
    tri = (np.arange(128)[None, :] >= np.arange(128)[:, None]) * 1.0
    c_bf = np.concatenate([np.ones((128, 128)), np.eye(128), tri, ctab], axis=1).astype(np.float32)
    return c_f32, c_bf


def host_params(p):
    L = DEPTH
    gc = np.stack([p["g_mix_pre"], p["g_mix_post"], p["g_ffn_pre"], p["g_ffn_post"]], 0)
    gcols = np.ascontiguousarray(gc.reshape(4, L, 16, 128).transpose(3, 0, 1, 2).reshape(128, 4 * L * 16))
    cw = p["conv_w"].reshape(L, CONVW, 4, 128).transpose(3, 0, 2, 1).reshape(128, L * 4 * CONVW)
    cb = p["conv_b"].reshape(L, 4, 128).transpose(2, 0, 1).reshape(128, L * 4)
    cn = p["conv_norm"].reshape(L, 4, 128).transpose(2, 0, 1).reshape(128, L * 4)
    gb = p["gmlp_b"].transpose(2, 0, 1).reshape(128, L * 8)
    p_cols = np.ascontiguousarray(np.concatenate([cw, cb, cn, gb], axis=1).astype(np.float32))
    lam = np.broadcast_to(p["diff_lam"].reshape(1, L * 256), (128, L * 256))
    sub = np.broadcast_to(p["diff_subln"].reshape(1, L * 128), (128, L * 128))
    p_rep = np.ascontiguousarray(np.concatenate([lam, sub], axis=1).astype(np.float32))
    return gcols, p_cols, p_rep


def make_in_maps(inputs, x_per_core):
    p = {k: np.asarray(v, dtype=np.float32) for k, v in inputs.items()}
    c_f32, c_bf = host_consts()
    gcols, p_cols, p_rep = host_params(p)
    maps = []
    for xc in x_per_core:
        maps.append({
            "flag": np.full((128, 1), float(len(maps) % 2), dtype=np.float32),
            "xT": np.ascontiguousarray(xc.T), "w_in": p["w_in"], "w_out": p["w_out"], "w_ff1": p["w_ff1"],
            "w_ff2": p["w_ff2"], "gmlp_w": np.ascontiguousarray(p["gmlp_w"].transpose(0, 3, 1, 2)), "gcols": gcols, "c_f32": c_f32, "c_bf": c_bf,
            "p_cols": p_cols, "p_rep": p_rep,
        })
    return maps


N_LAUNCH_LAYERS = 2


def kernel(**inputs):
    x = np.asarray(inputs["x"], dtype=np.float32)
    B = x.shape[0]
    cur = [x[c // 2, (c % 2) * SL:(c % 2 + 1) * SL] for c in range(8)]
    for l0 in range(0, DEPTH, N_LAUNCH_LAYERS):
        nc = build(N_LAUNCH_LAYERS, first_layer=l0)
        maps = make_in_maps(inputs, cur)
        res = run_bass_kernel_spmd(nc, maps, core_ids=list(range(8))).results
        cur = [np.ascontiguousarray(r["outT"].T) for r in res]
    out = np.stack([np.concatenate([cur[2 * b], cur[2 * b + 1]], axis=0) for b in range(B)], 0)
    return out.astype(np.float32)
```
